# Optimizing a Trainium2 kernel written in Bass

```python
import math
import jax
import jax.numpy as jnp
from jax import lax
import numpy as np

D_MODEL = 1024
BATCH = 2
SEQ = 8192
DEPTH = 4

NSA_HEADS = 8
NSA_KV_HEADS = 2
NSA_GROUP = NSA_HEADS // NSA_KV_HEADS
NSA_HEAD_DIM = 64
CMP_BLOCK = 32
CMP_STRIDE = 16
CMP_HIDDEN = 128
SLC_BLOCK = 64
N_SELECTED = 16
WINDOW = 512
Q_BLOCK = 128
GDN_HEADS = 4
GDN_HEAD_DIM = 128
GDN_CHUNK = 64
CONV_WIDTH = 4
D_FF = 2816
PLE_DIM = 256
ROPE_THETA = 10000.0
EPS = 1e-6
FORCE_SCORE = 1e6
NEG_INF = -1e30

NSA_WIDTH = NSA_HEADS * NSA_HEAD_DIM
NSA_KV_WIDTH = NSA_KV_HEADS * NSA_HEAD_DIM
GDN_WIDTH = GDN_HEADS * GDN_HEAD_DIM
D_MIX = NSA_WIDTH + GDN_WIDTH
IN_SIZES = (NSA_WIDTH, NSA_KV_WIDTH, NSA_KV_WIDTH, NSA_KV_WIDTH, NSA_KV_WIDTH,
            NSA_KV_WIDTH, NSA_KV_WIDTH, 3 * NSA_HEADS, 3 * GDN_WIDTH, GDN_WIDTH,
            GDN_HEADS, GDN_HEADS)
D_IN = sum(IN_SIZES)

kernel_name = 'hybrid_nsa_gdn_macaron_ple'


def rmsnorm(x, w):
    xf = x.astype(jnp.float32)
    y = xf * lax.rsqrt(jnp.mean(xf * xf, axis=-1, keepdims=True) + EPS)
    return (y * w.astype(jnp.float32)).astype(x.dtype)


def l2norm(x):
    return x * lax.rsqrt(jnp.sum(x * x, axis=-1, keepdims=True) + EPS)


def swiglu(h, w1, w3, w2):
    return (jax.nn.silu(h @ w1) * (h @ w3)) @ w2


def rope_tables(seq, dim):
    inv = 1.0 / (ROPE_THETA ** (jnp.arange(0, dim, 2, dtype=jnp.float32) / dim))
    ang = jnp.arange(seq, dtype=jnp.float32)[:, None] * inv[None, :]
    ang = jnp.concatenate([ang, ang], axis=-1)
    return jnp.cos(ang), jnp.sin(ang)


def apply_rope(x, cos, sin):
    half = x.shape[-1] // 2
    rot = jnp.concatenate([-x[..., half:], x[..., :half]], axis=-1)
    return x * cos[None, :, None, :].astype(x.dtype) + rot * sin[None, :, None, :].astype(x.dtype)


def masked_softmax(s, mask):
    s = jnp.where(mask, s.astype(jnp.float32), NEG_INF)
    return jnp.where(mask, jax.nn.softmax(s, axis=-1), 0.0)


def causal_conv(x, w):
    c = x.shape[-1]
    return lax.conv_general_dilated(
        x, w[:, None, :].astype(x.dtype), window_strides=(1,),
        padding=((CONV_WIDTH - 1, 0),), dimension_numbers=('NWC', 'WIO', 'NWC'),
        feature_group_count=c)


def compress(kv, pe, w1, w2):
    b, s, hkv, dh = kv.shape
    n_cmp = (s - CMP_BLOCK) // CMP_STRIDE + 1
    idx = jnp.arange(n_cmp)[:, None] * CMP_STRIDE + jnp.arange(CMP_BLOCK)[None, :]
    blk = kv[:, idx] + pe[None, None, :, None, :]
    flat = blk.transpose(0, 1, 3, 2, 4).reshape(b, n_cmp, hkv, CMP_BLOCK * dh)
    return jax.nn.silu(flat @ w1) @ w2


def nsa_attention(q, k_cmp, v_cmp, ks, vs, kw, vw, gates):
    b, s, _, dh = q.shape
    n_cmp = k_cmp.shape[1]
    n_slc = s // SLC_BLOCK
    n_sel = min(N_SELECTED, n_slc)
    n_qb = s // Q_BLOCK
    scale = dh ** -0.5
    cmp_end = jnp.arange(n_cmp) * CMP_STRIDE + CMP_BLOCK - 1
    jc = jnp.arange(n_cmp)[:, None]
    js = jnp.arange(n_slc)[None, :]
    overlap = ((jc * CMP_STRIDE < (js + 1) * SLC_BLOCK)
               & (jc * CMP_STRIDE + CMP_BLOCK > js * SLC_BLOCK)).astype(jnp.float32)
    blk_ids = jnp.arange(n_slc)
    ks_blk = ks.reshape(b, n_slc, SLC_BLOCK, NSA_KV_HEADS, dh).transpose(0, 3, 1, 2, 4)
    vs_blk = vs.reshape(b, n_slc, SLC_BLOCK, NSA_KV_HEADS, dh).transpose(0, 3, 1, 2, 4)
    kw_pad = jnp.pad(kw, ((0, 0), (WINDOW, 0), (0, 0), (0, 0)))
    vw_pad = jnp.pad(vw, ((0, 0), (WINDOW, 0), (0, 0), (0, 0)))
    q_all = (q * scale).reshape(b, n_qb, Q_BLOCK, NSA_KV_HEADS, NSA_GROUP, dh).transpose(1, 0, 2, 3, 4, 5)
    g_all = gates.reshape(b, n_qb, Q_BLOCK, NSA_KV_HEADS, NSA_GROUP, 3).transpose(1, 0, 2, 3, 4, 5)
    gather = jax.vmap(jax.vmap(lambda blocks, ids: blocks[ids]))

    def one_block(args):
        bi, qb, gb = args
        t = bi * Q_BLOCK + jnp.arange(Q_BLOCK)
        s_c = jnp.einsum('bqhgd,bnhd->bhgqn', qb, k_cmp)
        p_cmp = masked_softmax(s_c, cmp_end[None, :] <= t[:, None])
        o_cmp = jnp.einsum('bhgqn,bnhd->bqhgd', p_cmp.astype(v_cmp.dtype), v_cmp)
        imp = jnp.einsum('bhgqn,ns->bhqs', p_cmp, overlap)
        cur = t // SLC_BLOCK
        forced = ((blk_ids[None, :] == 0) | (blk_ids[None, :] == cur[:, None])
                  | (blk_ids[None, :] == cur[:, None] - 1))
        valid = blk_ids[None, :] * SLC_BLOCK <= t[:, None]
        score = jnp.where(forced, FORCE_SCORE, jnp.where(valid, imp, -1.0))
        _, sel = lax.top_k(score, n_sel)
        k_sel = gather(ks_blk, sel)
        v_sel = gather(vs_blk, sel)
        pos = sel[..., None] * SLC_BLOCK + jnp.arange(SLC_BLOCK)
        m_s = (pos <= t[None, None, :, None, None]).reshape(b, NSA_KV_HEADS, 1, Q_BLOCK, n_sel * SLC_BLOCK)
        s_s = jnp.einsum('bqhgd,bhqnld->bhgqnl', qb, k_sel).reshape(
            b, NSA_KV_HEADS, NSA_GROUP, Q_BLOCK, n_sel * SLC_BLOCK)
        p_s = masked_softmax(s_s, m_s).reshape(b, NSA_KV_HEADS, NSA_GROUP, Q_BLOCK, n_sel, SLC_BLOCK)
        o_slc = jnp.einsum('bhgqnl,bhqnld->bqhgd', p_s.astype(v_sel.dtype), v_sel)
        kwb = lax.dynamic_slice_in_dim(kw_pad, bi * Q_BLOCK, Q_BLOCK + WINDOW, axis=1)
        vwb = lax.dynamic_slice_in_dim(vw_pad, bi * Q_BLOCK, Q_BLOCK + WINDOW, axis=1)
        kpos = bi * Q_BLOCK - WINDOW + jnp.arange(Q_BLOCK + WINDOW)
        dist = t[:, None] - kpos[None, :]
        m_w = (kpos[None, :] >= 0) & (dist >= 0) & (dist < WINDOW)
        s_w = jnp.einsum('bqhgd,bkhd->bhgqk', qb, kwb)
        p_w = masked_softmax(s_w, m_w)
        o_win = jnp.einsum('bhgqk,bkhd->bqhgd', p_w.astype(vwb.dtype), vwb)
        return gb[..., 0:1] * o_cmp + gb[..., 1:2] * o_slc + gb[..., 2:3] * o_win

    out = lax.map(one_block, (jnp.arange(n_qb), q_all, g_all))
    return out.transpose(1, 0, 2, 3, 4, 5).reshape(b, s, NSA_WIDTH)


def gated_delta_rule(q, k, v, g, beta):
    b, h, s, dk = q.shape
    dv = v.shape[-1]
    c = GDN_CHUNK
    n = s // c
    q = q * dk ** -0.5
    q, k, v = [t.reshape(b, h, n, c, t.shape[-1]) for t in (q, k, v)]
    gc = jnp.cumsum(g.reshape(b, h, n, c), axis=-1)
    beta = beta.reshape(b, h, n, c)
    ii = jnp.arange(c)[:, None]
    jj = jnp.arange(c)[None, :]
    incl = ii >= jj
    strict = ii > jj
    decay = jnp.exp(jnp.where(incl, gc[..., :, None] - gc[..., None, :], -jnp.inf))
    kb = k * beta[..., None]
    a_s = jnp.where(strict, jnp.einsum('bhnik,bhnjk->bhnij', kb, k) * decay, 0.0)
    rhs = jnp.concatenate([v * beta[..., None], kb * jnp.exp(gc)[..., None]], axis=-1)
    sol = lax.linalg.triangular_solve(a_s, rhs, left_side=True, lower=True, unit_diagonal=True)
    u, w = sol[..., :dv], sol[..., dv:]
    attn = jnp.where(incl, jnp.einsum('bhnik,bhnjk->bhnij', q, k) * decay, 0.0)
    q_dec = q * jnp.exp(gc)[..., None]
    g_last = gc[..., -1]
    k_dec = k * jnp.exp(g_last[..., None] - gc)[..., None]
    xs = tuple(jnp.moveaxis(t, 2, 0) for t in (u, w, attn, q_dec, k_dec, g_last))

    def step(state, inp):
        u_n, w_n, a_n, qd, kd, gl = inp
        v_new = u_n - jnp.einsum('bhck,bhkv->bhcv', w_n, state)
        o = jnp.einsum('bhck,bhkv->bhcv', qd, state) + jnp.einsum('bhij,bhjv->bhiv', a_n, v_new)
        state = state * jnp.exp(gl)[..., None, None] + jnp.einsum('bhck,bhcv->bhkv', kd, v_new)
        return state, o

    _, o = lax.scan(step, jnp.zeros((b, h, dk, dv), jnp.float32), xs)
    return jnp.moveaxis(o, 0, 2).reshape(b, h, s, dv)


def hybrid_mixer(h, w_in, cmp_pe_k, cmp_pe_v, cmp_k_w1, cmp_k_w2, cmp_v_w1, cmp_v_w2,
                 gdn_conv, gdn_a_log, gdn_dt_bias, gdn_norm, w_out, cos, sin):
    b, s, _ = h.shape
    z = h @ w_in
    (q, kc, vc, ks, vs, kw, vw, g_nsa, qkv, zg, a_in, b_in) = jnp.split(
        z, np.cumsum(IN_SIZES)[:-1].tolist(), axis=-1)
    q = apply_rope(q.reshape(b, s, NSA_HEADS, NSA_HEAD_DIM), cos, sin)
    kc, ks, kw = [apply_rope(t.reshape(b, s, NSA_KV_HEADS, NSA_HEAD_DIM), cos, sin) for t in (kc, ks, kw)]
    vc, vs, vw = [t.reshape(b, s, NSA_KV_HEADS, NSA_HEAD_DIM) for t in (vc, vs, vw)]
    k_cmp = compress(kc, cmp_pe_k, cmp_k_w1, cmp_k_w2)
    v_cmp = compress(vc, cmp_pe_v, cmp_v_w1, cmp_v_w2)
    gates = jax.nn.sigmoid(g_nsa).reshape(b, s, NSA_HEADS, 3)
    o_nsa = nsa_attention(q, k_cmp, v_cmp, ks, vs, kw, vw, gates)
    qkv = jax.nn.silu(causal_conv(qkv, gdn_conv)).astype(jnp.float32)
    gq, gk, gv = jnp.split(qkv, 3, axis=-1)
    to_heads = lambda t: t.reshape(b, s, GDN_HEADS, GDN_HEAD_DIM).transpose(0, 2, 1, 3)
    gq, gk, gv = l2norm(to_heads(gq)), l2norm(to_heads(gk)), to_heads(gv)
    log_decay = -jnp.exp(gdn_a_log.astype(jnp.float32)) * jax.nn.softplus(
        a_in.astype(jnp.float32) + gdn_dt_bias.astype(jnp.float32))
    beta = jax.nn.sigmoid(b_in.astype(jnp.float32))
    o = gated_delta_rule(gq, gk, gv, log_decay.transpose(0, 2, 1), beta.transpose(0, 2, 1))
    o = o.transpose(0, 2, 1, 3)
    o = rmsnorm(o, gdn_norm) * jax.nn.silu(zg.reshape(b, s, GDN_HEADS, GDN_HEAD_DIM).astype(jnp.float32))
    o_gdn = o.reshape(b, s, GDN_WIDTH).astype(h.dtype)
    return jnp.concatenate([o_nsa, o_gdn], axis=-1) @ w_out


def setup_inputs(seed: int = 0) -> dict:
    key = jax.random.key(seed)
    k = jax.random.split(key, 28)
    L = DEPTH
    nrm = lambda kk, shape, fan: jax.random.normal(kk, shape, jnp.float32) * fan ** -0.5
    gain = lambda kk, shape: 1.0 + 0.02 * jax.random.normal(kk, shape, jnp.float32)
    dt = jnp.exp(jax.random.uniform(k[16], (L, GDN_HEADS), jnp.float32,
                                    minval=math.log(1e-3), maxval=math.log(1e-1)))
    return {
        'x': jax.random.normal(k[0], (BATCH, SEQ, D_MODEL), jnp.float32),
        'p': jax.random.normal(k[1], (DEPTH, BATCH, SEQ, PLE_DIM), jnp.float32),
        'ffn1_norm': gain(k[2], (L, D_MODEL)),
        'ffn1_w1': nrm(k[3], (L, D_MODEL, D_FF), D_MODEL),
        'ffn1_w3': nrm(k[4], (L, D_MODEL, D_FF), D_MODEL),
        'ffn1_w2': nrm(k[5], (L, D_FF, D_MODEL), D_FF),
        'mix_norm': gain(k[6], (L, D_MODEL)),
        'w_in': nrm(k[7], (L, D_MODEL, D_IN), D_MODEL),
        'cmp_pe_k': 0.02 * jax.random.normal(k[8], (L, CMP_BLOCK, NSA_HEAD_DIM), jnp.float32),
        'cmp_pe_v': 0.02 * jax.random.normal(k[9], (L, CMP_BLOCK, NSA_HEAD_DIM), jnp.float32),
        'cmp_k_w1': nrm(k[10], (L, CMP_BLOCK * NSA_HEAD_DIM, CMP_HIDDEN), CMP_BLOCK * NSA_HEAD_DIM),
        'cmp_k_w2': nrm(k[11], (L, CMP_HIDDEN, NSA_HEAD_DIM), CMP_HIDDEN),
        'cmp_v_w1': nrm(k[12], (L, CMP_BLOCK * NSA_HEAD_DIM, CMP_HIDDEN), CMP_BLOCK * NSA_HEAD_DIM),
        'cmp_v_w2': nrm(k[13], (L, CMP_HIDDEN, NSA_HEAD_DIM), CMP_HIDDEN),
        'gdn_conv': nrm(k[14], (L, CONV_WIDTH, 3 * GDN_WIDTH), CONV_WIDTH),
        'gdn_a_log': jnp.log(jax.random.uniform(k[15], (L, GDN_HEADS), jnp.float32, minval=1.0, maxval=16.0)),
        'gdn_dt_bias': dt + jnp.log(-jnp.expm1(-dt)),
        'gdn_norm': gain(k[17], (L, GDN_HEAD_DIM)),
        'w_out': nrm(k[18], (L, D_MIX, D_MODEL), D_MIX),
        'ffn2_norm': gain(k[19], (L, D_MODEL)),
        'ffn2_w1': nrm(k[20], (L, D_MODEL, D_FF), D_MODEL),
        'ffn2_w3': nrm(k[21], (L, D_MODEL, D_FF), D_MODEL),
        'ffn2_w2': nrm(k[22], (L, D_FF, D_MODEL), D_FF),
        'ple_norm': gain(k[23], (L, D_MODEL)),
        'ple_gate': nrm(k[24], (L, D_MODEL, D_MODEL), D_MODEL),
        'ple_proj': nrm(k[25], (L, PLE_DIM, D_MODEL), PLE_DIM),
        'final_norm': gain(k[26], (D_MODEL,)),
    }


def reference(x, p, ffn1_norm, ffn1_w1, ffn1_w3, ffn1_w2, mix_norm, w_in, cmp_pe_k, cmp_pe_v,
              cmp_k_w1, cmp_k_w2, cmp_v_w1, cmp_v_w2, gdn_conv, gdn_a_log, gdn_dt_bias, gdn_norm,
              w_out, ffn2_norm, ffn2_w1, ffn2_w3, ffn2_w2, ple_norm, ple_gate, ple_proj, final_norm):
    cos, sin = rope_tables(x.shape[1], NSA_HEAD_DIM)
    for i in range(DEPTH):
        x = x + 0.5 * swiglu(rmsnorm(x, ffn1_norm[i]), ffn1_w1[i], ffn1_w3[i], ffn1_w2[i])
        x = x + hybrid_mixer(rmsnorm(x, mix_norm[i]), w_in[i], cmp_pe_k[i], cmp_pe_v[i],
                             cmp_k_w1[i], cmp_k_w2[i], cmp_v_w1[i], cmp_v_w2[i], gdn_conv[i],
                             gdn_a_log[i], gdn_dt_bias[i], gdn_norm[i], w_out[i], cos, sin)
        x = x + 0.5 * swiglu(rmsnorm(x, ffn2_norm[i]), ffn2_w1[i], ffn2_w3[i], ffn2_w2[i])
        gate = jax.nn.sigmoid(rmsnorm(x, ple_norm[i]) @ ple_gate[i])
        x = x + gate * (p[i] @ ple_proj[i])
    return rmsnorm(x, final_norm)
```

```python
import contextlib
import numpy as np
import ml_dtypes
import concourse.bass as bass
import concourse.mybir as mybir
from concourse.bass_utils import run_bass_kernel_spmd

F32 = mybir.dt.float32
BF16 = mybir.dt.bfloat16
AF = mybir.ActivationFunctionType
ALU = mybir.AluOpType

NCORES = 8
D = 1024
DFF = 2816
NFF = DFF // 128
S = 8192
B = 2
DEPTH = 4
TOK = 2048
HALF = 1024
EPS = 1e-6
D_IN = 3360
FF_GROUPS = [(0, 6), (6, 12), (12, 17), (17, 22)]
NFM = 20
NTM = 800


class Buf:
    __slots__ = ("name", "w", "rs", "excl")

    def __init__(self, name, excl=False):
        self.name = name
        self.excl = excl
        self.w = None
        self.rs = {}


class Eng:
    def __init__(self, k, name, e, selfsync, ndma=0):
        self.k = k
        self.name = name
        self.e = e
        self.sem = k.newsem(name)
        self.cnt = 0
        self.seen = {}
        self.selfsync = selfsync
        self.dsems = [k.newsem(f"{name}_d{i}") for i in range(ndma)]
        self.dj = 0


class K:
    def __init__(self, nc):
        self.nc = nc
        self.es = contextlib.ExitStack()
        self.sems = []
        self.pe = Eng(self, "pe", nc.tensor, False)
        self.act = Eng(self, "act", nc.scalar, True)
        self.dve = Eng(self, "dve", nc.vector, True)
        self.pool = Eng(self, "pool", nc.gpsimd, True, ndma=8)
        self.sp = Eng(self, "sp", nc.sync, False, ndma=8)
        self.engs = [self.pe, self.act, self.dve, self.pool, self.sp]
        self.nbuf = 0

    def newsem(self, name):
        s = self.nc.alloc_semaphore(name=name)
        self.sems.append(s)
        return s

    def sb(self, name, shape, dt):
        t = self.es.enter_context(self.nc.sbuf_tensor("s_" + name, shape, dt))
        return t

    def ps(self, name, shape, dt):
        t = self.es.enter_context(self.nc.psum_tensor("p_" + name, shape, dt))
        return t

    def buf(self, name="b", excl=False):
        self.nbuf += 1
        return Buf(f"{name}{self.nbuf}", excl)

    def _wait(self, eng, reads, writes, extra=()):
        need = {}

        def add(sem, val, owner):
            if owner is eng and not eng.selfsync:
                return
            if eng.seen.get(sem.num, 0) >= val:
                return
            if sem.num not in need or need[sem.num][1] < val:
                need[sem.num] = (sem, val)

        for b in reads:
            if b.w is not None:
                add(*b.w)
            if b.excl:
                for (sem, val, owner) in b.rs.values():
                    if owner is not eng:
                        add(sem, val, owner)
        for b in writes:
            if b.w is not None:
                add(*b.w)
            for (sem, val, owner) in b.rs.values():
                add(sem, val, owner)
        for ev in extra:
            add(*ev)
        for sem, val in need.values():
            eng.e.wait_ge(sem, val)
            eng.seen[sem.num] = val

    def _record(self, ev, reads, writes):
        for b in writes:
            b.w = ev
            b.rs = {}
        for b in reads:
            if b in writes:
                continue
            b.rs[ev[0].num] = ev

    def op(self, eng, fn, reads=(), writes=()):
        self._wait(eng, reads, writes)
        inst = fn()
        eng.cnt += 1
        inst.then_inc(eng.sem, 1)
        ev = (eng.sem, eng.cnt, eng)
        self._record(ev, reads, writes)
        return ev

    def dma(self, eng, fn, reads=(), writes=()):
        n = len(eng.dsems)
        slot = eng.dj % n
        tgt = 16 * (eng.dj // n + 1)
        sem = eng.dsems[slot]
        extra = [(sem, tgt - 16, None)] if tgt > 16 else []
        self._wait(eng, reads, writes, extra)
        inst = fn()
        inst.then_inc(sem, 16)
        eng.dj += 1
        ev = (sem, tgt, None)
        self._record(ev, reads, writes)
        return ev

    def finish(self, final_bufs):
        pool = self.pool
        evs = []
        for e in self.engs:
            if e.cnt > 0 and e is not pool:
                evs.append((e.sem, e.cnt, e))
            n = len(e.dsems)
            for i, s in enumerate(e.dsems):
                cnt = (e.dj - i + n - 1) // n if e.dj > i else 0
                if cnt > 0:
                    evs.append((s, 16 * cnt, None))
        self._wait(pool, (), (), evs)
        self.nc.all_engine_barrier()
        for s in self.sems:
            self.nc.gpsimd.sem_clear(s)
        self.nc.all_engine_barrier()
        self.es.close()


def bc(ap, shape):
    return ap.broadcast_to(shape)


class TCtx:
    def __init__(self, k):
        nc = k.nc
        self.k = k
        self.x = k.sb("x", [128, 8, D], F32)
        self.xb = [k.buf("x") for _ in range(8)]
        self.hT = k.sb("hT", [128, 8, HALF], BF16)
        self.hTb = [k.buf("hT") for _ in range(8)]
        self.hb = [k.sb(f"hb{i}", [128, D], BF16) for i in range(2)]
        self.hbb = [k.buf("hb") for _ in range(2)]
        self.junk = k.sb("junk", [128, D], BF16)
        self.junkb = k.buf("junk")
        self.stat = k.sb("stat", [128, 64], F32)
        self.statb = [k.buf("stat") for _ in range(16)]
        self.pst = k.ps("pst", [128, 4096], F32)
        self.psb = [k.buf("ps", excl=True) for _ in range(8)]
        self.ident = k.sb("ident", [128, 128], BF16)
        self.identb = k.buf("ident")
        self.aT = [k.sb(f"aT{i}", [128, 6, HALF], BF16) for i in range(2)]
        self.aTb = [[[k.buf("aT") for _ in range(2)] for _ in range(6)] for _ in range(2)]
        self.w2 = [k.sb(f"w2_{i}", [128, 6, D], BF16) for i in range(2)]
        self.w2b = [[k.buf("w2") for _ in range(6)] for _ in range(2)]
        self.w13 = [[k.sb(f"w13_{a}{i}", [128, 8, 128], BF16) for i in range(2)] for a in range(2)]
        self.w13b = [[k.buf("w13") for _ in range(2)] for _ in range(2)]
        self.silu = [k.sb(f"silu{i}", [128, 512], F32) for i in range(2)]
        self.silub = [k.buf("silu") for _ in range(2)]
        self.normw = k.sb("normw", [128, 8, 8], F32)
        self.normwb = [k.buf("normw") for _ in range(8)]
        self.cnt = {"w13": 0, "silu": 0, "hb": 0, "stat": 0, "grp": 0, "uv": 0, "y": 0, "tp": 0}

    def bank(self, i, n=512):
        return self.pst[:, i * 512:i * 512 + n]


def emit_ident(c, ident_dram):
    k, nc = c.k, c.k.nc
    k.dma(k.sp, lambda: nc.sync.dma_start(out=c.ident[:], in_=ident_dram[:, :]), writes=[c.identb])


def emit_load_normw(c, slot, vec_dram):
    k, nc = c.k, c.k.nc
    with nc.allow_non_contiguous_dma(reason="small norm vector"):
        k.dma(k.sp, lambda: nc.sync.dma_start(out=c.normw[:, :, slot],
                                              in_=vec_dram.rearrange("(kt p) -> p kt", p=128)),
              writes=[c.normwb[slot]])


def emit_load_x(c, x_dram, half):
    k, nc = c.k, c.k.nc
    for t in range(8):
        r0 = half * HALF + t * 128
        k.dma(k.sp, lambda: nc.sync.dma_start(out=c.x[:, t, :], in_=x_dram[r0:r0 + 128, :]),
              writes=[c.xb[t]])


def emit_store_x(c, x_dram, half):
    k, nc = c.k, c.k.nc
    for t in range(8):
        r0 = half * HALF + t * 128
        k.dma(k.sp, lambda: nc.sync.dma_start(out=x_dram[r0:r0 + 128, :], in_=c.x[:, t, :]),
              reads=[c.xb[t]])


def emit_rstd(c, t):
    k, nc = c.k, c.k.nc
    si = c.cnt["stat"] % 16
    c.cnt["stat"] += 1
    sb = c.statb[si]
    ss = c.stat[:, 4 * si:4 * si + 1]
    sd = c.stat[:, 4 * si + 1:4 * si + 2]
    rs = c.stat[:, 4 * si + 2:4 * si + 3]
    k.op(k.act, lambda: nc.scalar.activation(out=c.junk[:], in_=c.x[:, t, :], func=AF.Square, accum_out=ss),
         reads=[c.xb[t]], writes=[c.junkb, sb])
    k.op(k.act, lambda: nc.scalar.activation(out=sd, in_=ss, func=AF.Sqrt, scale=1.0 / D, bias=c.epsap),
         reads=[sb], writes=[sb])
    k.op(k.dve, lambda: nc.vector.reciprocal(out=rs, in_=sd), reads=[sb], writes=[sb])
    return rs, sb


def emit_normT(c, slot):
    k, nc = c.k, c.k.nc
    for t in range(8):
        rs, sb = emit_rstd(c, t)
        hi = c.cnt["hb"] % 2
        c.cnt["hb"] += 1
        hb, hbb = c.hb[hi], c.hbb[hi]
        k.op(k.dve, lambda: nc.vector.tensor_scalar(out=hb[:], in0=c.x[:, t, :], scalar1=rs, scalar2=None,
                                                    op0=ALU.mult),
             reads=[c.xb[t], sb], writes=[hbb])
        bi = 4 + (c.cnt["tp"] % 4)
        c.cnt["tp"] += 1
        pb = c.psb[bi]
        pst_bf = c.bank(bi).bitcast(BF16)
        for kt in range(8):
            k.op(k.pe, lambda: nc.tensor.transpose(out=pst_bf[:, kt * 128:(kt + 1) * 128],
                                                   in_=hb[:, kt * 128:(kt + 1) * 128], identity=c.ident[:]),
                 reads=[hbb, c.identb], writes=[pb])
        k.op(k.dve, lambda: nc.vector.tensor_tensor(
            out=c.hT[:, :, t * 128:(t + 1) * 128],
            in0=pst_bf.rearrange("p (kt m) -> p kt m", kt=8),
            in1=bc(c.normw[:, :, slot:slot + 1], [128, 8, 128]), op=ALU.mult),
             reads=[pb, c.normwb[slot]], writes=[c.hTb[t]])


def emit_ffn(c, w1d, w3d, w2d, scale):
    k, nc = c.k, c.k.nc
    w1v = w1d.rearrange("(kt p) f -> p kt f", p=128)
    w3v = w3d.rearrange("(kt p) f -> p kt f", p=128)
    for (j0, j1) in FF_GROUPS:
        gi = c.cnt["grp"] % 2
        c.cnt["grp"] += 1
        for jj, j in enumerate(range(j0, j1)):
            k.dma(k.pool, lambda: nc.gpsimd.dma_start(out=c.w2[gi][:, jj, :], in_=w2d[j * 128:(j + 1) * 128, :]),
                  writes=[c.w2b[gi][jj]])
        for jj, j in enumerate(range(j0, j1)):
            wi = c.cnt["w13"] % 2
            c.cnt["w13"] += 1
            k.dma(k.pool, lambda: nc.gpsimd.dma_start(out=c.w13[0][wi][:], in_=w1v[:, :, j * 128:(j + 1) * 128]),
                  writes=[c.w13b[0][wi]])
            k.dma(k.pool, lambda: nc.gpsimd.dma_start(out=c.w13[1][wi][:], in_=w3v[:, :, j * 128:(j + 1) * 128]),
                  writes=[c.w13b[1][wi]])
            for ch in range(2):
                ui = c.cnt["uv"] % 2
                c.cnt["uv"] += 1
                bu, bv = 2 * ui, 2 * ui + 1
                for a, bnk in ((0, bu), (1, bv)):
                    for kt in range(8):
                        k.op(k.pe, lambda: nc.tensor.matmul(c.bank(bnk), lhsT=c.w13[a][wi][:, kt, :],
                                                            rhs=c.hT[:, kt, ch * 512:(ch + 1) * 512],
                                                            start=(kt == 0), stop=(kt == 7)),
                             reads=[c.w13b[a][wi]] + c.hTb[ch * 4:(ch + 1) * 4], writes=[c.psb[bnk]])
                si = c.cnt["silu"] % 2
                c.cnt["silu"] += 1
                k.op(k.act, lambda: nc.scalar.activation(out=c.silu[si][:], in_=c.bank(bu), func=AF.Silu),
                     reads=[c.psb[bu]], writes=[c.silub[si]])
                k.op(k.dve, lambda: nc.vector.tensor_tensor(out=c.aT[gi][:, jj, ch * 512:(ch + 1) * 512],
                                                            in0=c.silu[si][:], in1=c.bank(bv), op=ALU.mult),
                     reads=[c.silub[si], c.psb[bv]], writes=[c.aTb[gi][jj][ch]])
        ng = j1 - j0
        for t in range(8):
            yi = c.cnt["y"] % 2
            c.cnt["y"] += 1
            b0 = 4 + 2 * yi
            for hh in range(2):
                for jj in range(ng):
                    k.op(k.pe, lambda: nc.tensor.matmul(c.bank(b0 + hh), lhsT=c.aT[gi][:, jj, t * 128:(t + 1) * 128],
                                                        rhs=c.w2[gi][:, jj, hh * 512:(hh + 1) * 512],
                                                        start=(jj == 0), stop=(jj == ng - 1)),
                         reads=[c.aTb[gi][jj][t // 4], c.w2b[gi][jj]], writes=[c.psb[b0 + hh]])
            k.op(k.dve, lambda: nc.vector.scalar_tensor_tensor(out=c.x[:, t, :], in0=c.pst[:, b0 * 512:(b0 + 2) * 512],
                                                               scalar=float(scale), in1=c.x[:, t, :],
                                                               op0=ALU.mult, op1=ALU.add),
                 reads=[c.psb[b0], c.psb[b0 + 1], c.xb[t]], writes=[c.xb[t]])


def emit_eps(c):
    k, nc = c.k, c.k.nc
    c.epst = k.sb("epst", [128, 1], F32)
    c.epsb = k.buf("eps")
    c.epsap = c.epst[:, 0:1]
    k.op(k.pool, lambda: nc.gpsimd.memset(c.epst[:], EPS), writes=[c.epsb])
    k._wait(k.act, [c.epsb], [])


FM_TILES = [(0, True), (128, True), (256, True), (384, True), (512, True), (640, False), (768, True),
            (1024, True)] + [(1304 + 128 * i, False) for i in range(12)]
TM_COLS = [(896, 128, 0), (1152, 128, 128), (1280, 24, 256), (2840, 512, 280), (3352, 8, 792)]


def build_TA():
    nc = bass.Bass("TRN2", target_bir_lowering=False)
    xd = nc.dram_tensor("x", [TOK, D], F32, kind="ExternalInput").ap()
    identd = nc.dram_tensor("ident", [128, 128], BF16, kind="ExternalInput").ap()
    cosd = nc.dram_tensor("cosT", [128, TOK], F32, kind="ExternalInput").ap()
    sind = nc.dram_tensor("sinT", [128, TOK], F32, kind="ExternalInput").ap()
    n1d = nc.dram_tensor("ffn_norm", [D], F32, kind="ExternalInput").ap()
    w1d = nc.dram_tensor("w1", [D, DFF], F32, kind="ExternalInput").ap()
    w3d = nc.dram_tensor("w3", [D, DFF], F32, kind="ExternalInput").ap()
    w2d = nc.dram_tensor("w2", [DFF, D], F32, kind="ExternalInput").ap()
    n2d = nc.dram_tensor("mix_norm", [D], F32, kind="ExternalInput").ap()
    wind = nc.dram_tensor("w_in", [D, D_IN], F32, kind="ExternalInput").ap()
    xo = nc.dram_tensor("x_out", [TOK, D], F32, kind="ExternalOutput").ap()
    zfm = nc.dram_tensor("zfm", [NFM * 128, TOK], BF16, kind="ExternalOutput").ap()
    ztm = nc.dram_tensor("ztm", [TOK, NTM], F32, kind="ExternalOutput").ap()

    k = K(nc)
    c = TCtx(k)
    emit_eps(c)
    emit_ident(c, identd)
    emit_load_normw(c, 0, n1d)
    emit_load_normw(c, 1, n2d)
    cos = k.sb("cos", [128, HALF], F32)
    sin = k.sb("sin", [128, HALF], F32)
    csb = k.buf("cs")
    wtm = k.sb("wtm", [128, 8, NTM], BF16)
    wtmb = k.buf("wtm")
    wfm = [k.sb(f"wfm{i}", [128, 8, 128], BF16) for i in range(2)]
    wfmb = [k.buf("wfm") for _ in range(2)]
    wrot = [k.sb(f"wrot{i}", [128, 8, 128], BF16) for i in range(2)]
    wrotb = [k.buf("wrot") for _ in range(2)]
    t1 = k.sb("t1", [128, 512], F32)
    t2 = k.sb("t2", [128, 512], F32)
    t1b, t2b = k.buf("t1"), k.buf("t2")
    sfm = [k.sb(f"sfm{i}", [128, 512], BF16) for i in range(2)]
    sfmb = [k.buf("sfm") for _ in range(2)]
    stm = [k.sb(f"stm{i}", [128, NTM], F32) for i in range(2)]
    stmb = [k.buf("stm") for _ in range(2)]
    winv = wind.rearrange("(kt p) f -> p kt f", p=128)
    for (s0, w, d0) in TM_COLS:
        k.dma(k.pool, lambda: nc.gpsimd.dma_start(out=wtm[:, :, d0:d0 + w], in_=winv[:, :, s0:s0 + w]), writes=[wtmb])

    nfm = 0
    ntm = 0
    for half in range(2):
        emit_load_x(c, xd, half)
        k.dma(k.sp, lambda: nc.sync.dma_start(out=cos[:], in_=cosd[:, half * HALF:(half + 1) * HALF]), writes=[csb])
        k.dma(k.sp, lambda: nc.sync.dma_start(out=sin[:], in_=sind[:, half * HALF:(half + 1) * HALF]), writes=[csb])
        emit_normT(c, 0)
        emit_ffn(c, w1d, w3d, w2d, 0.5)
        emit_store_x(c, xo, half)
        emit_normT(c, 1)
        for ti, (c0, rope) in enumerate(FM_TILES):
            wi = nfm % 2
            nfm += 1
            k.dma(k.pool, lambda: nc.gpsimd.dma_start(out=wfm[wi][:], in_=winv[:, :, c0:c0 + 128]), writes=[wfmb[wi]])
            if rope:
                src = wfm[wi][:].rearrange("p k (h two d) -> p k h two d", h=2, two=2)
                dst = wrot[wi][:].rearrange("p k (h two d) -> p k h two d", h=2, two=2)
                k.op(k.act, lambda: nc.scalar.mul(out=dst[:, :, :, 0, :], in_=src[:, :, :, 1, :], mul=-1.0),
                     reads=[wfmb[wi]], writes=[wrotb[wi]])
                k.op(k.act, lambda: nc.scalar.copy(out=dst[:, :, :, 1, :], in_=src[:, :, :, 0, :]),
                     reads=[wfmb[wi]], writes=[wrotb[wi]])
            for ch in range(2):
                ui = c.cnt["uv"] % 2
                c.cnt["uv"] += 1
                bz, br = 2 * ui, 2 * ui + 1
                for kt in range(8):
                    k.op(k.pe, lambda: nc.tensor.matmul(c.bank(bz), lhsT=wfm[wi][:, kt, :],
                                                        rhs=c.hT[:, kt, ch * 512:(ch + 1) * 512],
                                                        start=(kt == 0), stop=(kt == 7)),
                         reads=[wfmb[wi]] + c.hTb[ch * 4:(ch + 1) * 4], writes=[c.psb[bz]])
                if rope:
                    for kt in range(8):
                        k.op(k.pe, lambda: nc.tensor.matmul(c.bank(br), lhsT=wrot[wi][:, kt, :],
                                                            rhs=c.hT[:, kt, ch * 512:(ch + 1) * 512],
                                                            start=(kt == 0), stop=(kt == 7)),
                             reads=[wrotb[wi]] + c.hTb[ch * 4:(ch + 1) * 4], writes=[c.psb[br]])
                si = nfm % 2 if False else (c.cnt["silu"] % 2)
                c.cnt["silu"] += 1
                if rope:
                    k.op(k.dve, lambda: nc.vector.tensor_tensor(out=t1[:], in0=c.bank(bz),
                                                                in1=cos[:, ch * 512:(ch + 1) * 512], op=ALU.mult),
                         reads=[c.psb[bz], csb], writes=[t1b])
                    k.op(k.dve, lambda: nc.vector.tensor_tensor(out=t2[:], in0=c.bank(br),
                                                                in1=sin[:, ch * 512:(ch + 1) * 512], op=ALU.mult),
                         reads=[c.psb[br], csb], writes=[t2b])
                    k.op(k.pool, lambda: nc.gpsimd.tensor_tensor(out=sfm[si][:], in0=t1[:], in1=t2[:], op=ALU.add),
                         reads=[t1b, t2b], writes=[sfmb[si]])
                else:
                    k.op(k.act, lambda: nc.scalar.copy(out=sfm[si][:], in_=c.bank(bz)),
                         reads=[c.psb[bz]], writes=[sfmb[si]])
                col0 = half * HALF + ch * 512
                k.dma(k.sp, lambda: nc.sync.dma_start(out=zfm[ti * 128:(ti + 1) * 128, col0:col0 + 512], in_=sfm[si][:]),
                      reads=[sfmb[si]])
        for t in range(8):
            yi = c.cnt["y"] % 2
            c.cnt["y"] += 1
            b0 = 4 + 2 * yi
            for n2 in range(2):
                for kt in range(8):
                    k.op(k.pe, lambda: nc.tensor.matmul(c.bank(b0 + n2, 400), lhsT=c.hT[:, kt, t * 128:(t + 1) * 128],
                                                        rhs=wtm[:, kt, n2 * 400:(n2 + 1) * 400],
                                                        start=(kt == 0), stop=(kt == 7)),
                         reads=[c.hTb[t], wtmb], writes=[c.psb[b0 + n2]])
            si = ntm % 2
            ntm += 1
            for n2 in range(2):
                k.op(k.act, lambda: nc.scalar.copy(out=stm[si][:, n2 * 400:(n2 + 1) * 400], in_=c.bank(b0 + n2, 400)),
                     reads=[c.psb[b0 + n2]], writes=[stmb[si]])
            r0 = half * HALF + t * 128
            k.dma(k.sp, lambda: nc.sync.dma_start(out=ztm[r0:r0 + 128, :], in_=stm[si][:]), reads=[stmb[si]])
    k.finish([])
    return nc


def rope_tables_fm():
    inv = (1.0 / (10000.0 ** (np.arange(0, 64, 2, dtype=np.float32) / np.float32(64.0)))).astype(np.float32)
    ang = np.arange(S, dtype=np.float32)[:, None] * inv[None, :]
    ang = np.concatenate([ang, ang], axis=-1)
    cos = np.cos(ang).astype(np.float32).T
    sin = np.sin(ang).astype(np.float32).T
    return np.ascontiguousarray(np.concatenate([cos, cos], 0)), np.ascontiguousarray(np.concatenate([sin, sin], 0))


def ident_bf16():
    return np.eye(128, dtype=np.float32).astype(ml_dtypes.bfloat16)


_PROGS = {}


def get_prog(name):
    if name not in _PROGS:
        _PROGS[name] = {"TA": build_TA}[name]()
    return _PROGS[name]


def build_TB():
    nc = bass.Bass("TRN2", target_bir_lowering=False)
    xd = nc.dram_tensor("x", [TOK, D], F32, kind="ExternalInput").ap()
    omd = nc.dram_tensor("omixT", [D, TOK], BF16, kind="ExternalInput").ap()
    identd = nc.dram_tensor("ident", [128, 128], BF16, kind="ExternalInput").ap()
    woutd = nc.dram_tensor("w_out", [D, D], F32, kind="ExternalInput").ap()
    n1d = nc.dram_tensor("ffn_norm", [D], F32, kind="ExternalInput").ap()
    w1d = nc.dram_tensor("w1", [D, DFF], F32, kind="ExternalInput").ap()
    w3d = nc.dram_tensor("w3", [D, DFF], F32, kind="ExternalInput").ap()
    w2d = nc.dram_tensor("w2", [DFF, D], F32, kind="ExternalInput").ap()
    n2d = nc.dram_tensor("ple_norm", [D], F32, kind="ExternalInput").ap()
    wgd = nc.dram_tensor("ple_gate", [D, D], F32, kind="ExternalInput").ap()
    wpd = nc.dram_tensor("ple_proj", [256, D], F32, kind="ExternalInput").ap()
    pd = nc.dram_tensor("p", [TOK, 256], F32, kind="ExternalInput").ap()
    fnd = nc.dram_tensor("final_norm", [D], F32, kind="ExternalInput").ap()
    xo = nc.dram_tensor("x_out", [TOK, D], F32, kind="ExternalOutput").ap()
    xno = nc.dram_tensor("xn_out", [TOK, D], F32, kind="ExternalOutput").ap()

    k = K(nc)
    c = TCtx(k)
    emit_eps(c)
    emit_ident(c, identd)
    emit_load_normw(c, 0, n1d)
    emit_load_normw(c, 1, n2d)
    wsq = k.sb("wsq", [128, 8, D], BF16)
    wsqb = k.buf("wsq")
    omT = k.sb("omT", [128, 8, HALF], BF16)
    omTb = k.buf("omT")
    wp = k.sb("wp", [128, 2, D], BF16)
    wpb = k.buf("wp")
    fw = k.sb("fw", [128, D], F32)
    fwb = k.buf("fw")
    gate = [k.sb(f"gate{i}", [128, D], F32) for i in range(2)]
    gateb = [k.buf("gate") for _ in range(2)]
    pf = k.sb("pf", [128, 256], F32)
    pfb = k.buf("pf")
    pbf = k.sb("pbf", [128, 256], BF16)
    pbfb = k.buf("pbf")
    pT = k.sb("pT", [128, 2, 128], BF16)
    pTb = k.buf("pT")
    tmp = [k.sb(f"tmp{i}", [128, D], F32) for i in range(2)]
    tmpb = [k.buf("tmp") for _ in range(2)]
    k.dma(k.pool, lambda: nc.gpsimd.dma_start(out=wp[:], in_=wpd.rearrange("(kk p) f -> p kk f", p=128)), writes=[wpb])
    k.dma(k.sp, lambda: nc.sync.dma_start(out=fw[:], in_=fnd.partition_broadcast(128)), writes=[fwb])
    wov = woutd.rearrange("(kt p) f -> p kt f", p=128)
    wgv = wgd.rearrange("(kt p) f -> p kt f", p=128)
    ng = 0
    for half in range(2):
        emit_load_x(c, xd, half)
        for kt in range(8):
            k.dma(k.sp, lambda: nc.sync.dma_start(out=omT[:, kt, :], in_=omd[kt * 128:(kt + 1) * 128, half * HALF:(half + 1) * HALF]),
                  writes=[omTb])
        for kt in range(8):
            k.dma(k.pool, lambda: nc.gpsimd.dma_start(out=wsq[:, kt, :], in_=wov[:, kt, :]), writes=[wsqb])
        for t in range(8):
            yi = c.cnt["y"] % 2
            c.cnt["y"] += 1
            b0 = 4 + 2 * yi
            for hh in range(2):
                for kt in range(8):
                    k.op(k.pe, lambda: nc.tensor.matmul(c.bank(b0 + hh), lhsT=omT[:, kt, t * 128:(t + 1) * 128],
                                                        rhs=wsq[:, kt, hh * 512:(hh + 1) * 512],
                                                        start=(kt == 0), stop=(kt == 7)),
                         reads=[omTb, wsqb], writes=[c.psb[b0 + hh]])
            k.op(k.dve, lambda: nc.vector.tensor_tensor(out=c.x[:, t, :], in0=c.pst[:, b0 * 512:(b0 + 2) * 512],
                                                        in1=c.x[:, t, :], op=ALU.add),
                 reads=[c.psb[b0], c.psb[b0 + 1], c.xb[t]], writes=[c.xb[t]])
        emit_normT(c, 0)
        emit_ffn(c, w1d, w3d, w2d, 0.5)
        emit_normT(c, 1)
        for kt in range(8):
            k.dma(k.pool, lambda: nc.gpsimd.dma_start(out=wsq[:, kt, :], in_=wgv[:, kt, :]), writes=[wsqb])
        for t in range(8):
            r0 = half * HALF + t * 128
            gi = ng % 2
            ng += 1
            yi = c.cnt["y"] % 2
            c.cnt["y"] += 1
            b0 = 4 + 2 * yi
            for hh in range(2):
                for kt in range(8):
                    k.op(k.pe, lambda: nc.tensor.matmul(c.bank(b0 + hh), lhsT=c.hT[:, kt, t * 128:(t + 1) * 128],
                                                        rhs=wsq[:, kt, hh * 512:(hh + 1) * 512],
                                                        start=(kt == 0), stop=(kt == 7)),
                         reads=[c.hTb[t], wsqb], writes=[c.psb[b0 + hh]])
            k.op(k.act, lambda: nc.scalar.activation(out=gate[gi][:], in_=c.pst[:, b0 * 512:(b0 + 2) * 512], func=AF.Sigmoid),
                 reads=[c.psb[b0], c.psb[b0 + 1]], writes=[gateb[gi]])
            k.dma(k.sp, lambda: nc.sync.dma_start(out=pf[:], in_=pd[r0:r0 + 128, :]), writes=[pfb])
            k.op(k.dve, lambda: nc.vector.tensor_copy(out=pbf[:], in_=pf[:]), reads=[pfb], writes=[pbfb])
            ui = c.cnt["uv"] % 2
            c.cnt["uv"] += 1
            bt = 2 * ui
            ptb = c.bank(bt).bitcast(BF16)
            for kk in range(2):
                k.op(k.pe, lambda: nc.tensor.transpose(out=ptb[:, kk * 128:(kk + 1) * 128], in_=pbf[:, kk * 128:(kk + 1) * 128],
                                                       identity=c.ident[:]),
                     reads=[pbfb, c.identb], writes=[c.psb[bt]])
            k.op(k.act, lambda: nc.scalar.copy(out=pT[:].rearrange("p a b -> p (a b)"), in_=ptb[:, 0:256]),
                 reads=[c.psb[bt]], writes=[pTb])
            ui = c.cnt["uv"] % 2
            c.cnt["uv"] += 1
            bp = 2 * ui
            for hh in range(2):
                for kk in range(2):
                    k.op(k.pe, lambda: nc.tensor.matmul(c.bank(bp + hh), lhsT=pT[:, kk, :],
                                                        rhs=wp[:, kk, hh * 512:(hh + 1) * 512],
                                                        start=(kk == 0), stop=(kk == 1)),
                         reads=[pTb, wpb], writes=[c.psb[bp + hh]])
            k.op(k.dve, lambda: nc.vector.tensor_tensor(out=tmp[0][:], in0=gate[gi][:], in1=c.pst[:, bp * 512:(bp + 2) * 512],
                                                        op=ALU.mult),
                 reads=[gateb[gi], c.psb[bp], c.psb[bp + 1]], writes=[tmpb[0]])
            k.op(k.pool, lambda: nc.gpsimd.tensor_tensor(out=c.x[:, t, :], in0=c.x[:, t, :], in1=tmp[0][:], op=ALU.add),
                 reads=[tmpb[0], c.xb[t]], writes=[c.xb[t]])
            rs, sb = emit_rstd(c, t)
            k.op(k.dve, lambda: nc.vector.tensor_scalar(out=tmp[1][:], in0=c.x[:, t, :], scalar1=rs, scalar2=None, op0=ALU.mult),
                 reads=[c.xb[t], sb], writes=[tmpb[1]])
            k.op(k.pool, lambda: nc.gpsimd.tensor_tensor(out=tmp[1][:], in0=tmp[1][:], in1=fw[:], op=ALU.mult),
                 reads=[fwb, tmpb[1]], writes=[tmpb[1]])
            k.dma(k.sp, lambda: nc.sync.dma_start(out=xno[r0:r0 + 128, :], in_=tmp[1][:]), reads=[tmpb[1]])
        emit_store_x(c, xo, half)
    k.finish([])
    return nc


NEG = -30000.0
NQB = 32
NQ = NQB * 128


class MCtx:
    def __init__(self, k):
        self.k = k
        self.pst = k.ps("pst", [128, 4096], F32)
        self.psb = [k.buf("ps", excl=True) for _ in range(8)]
        self.big = [k.sb(f"big{i}", [128, S], BF16) for i in range(4)]
        self.bigb = [k.buf("big") for _ in range(4)]
        self.ident = k.sb("ident", [128, 128], BF16)
        self.identb = k.buf("ident")
        self.cst = k.sb("cst", [128, 4], F32)
        self.cstb = k.buf("cst")
        self.scr = k.sb("scr", [128, S], BF16)
        self.scr_bufs = []

    def bank(self, i, n=512, off=0):
        return self.pst[:, i * 512 + off:i * 512 + off + n]


def emit_mconsts(c, identd):
    k, nc = c.k, c.k.nc
    k.dma(k.sp, lambda: nc.sync.dma_start(out=c.ident[:], in_=identd[:, :]), writes=[c.identb])
    k.op(k.pool, lambda: nc.gpsimd.memset(c.cst[:, 0:1], EPS), writes=[c.cstb])
    k.op(k.pool, lambda: nc.gpsimd.memset(c.cst[:, 1:2], 1.0), writes=[c.cstb])
    k.op(k.pool, lambda: nc.gpsimd.memset(c.cst[:, 2:3], 1e-30), writes=[c.cstb])
    k._wait(k.act, [c.cstb], [])
    k._wait(k.dve, [c.cstb], [])


def emit_nsa(c, d):
    k, nc = c.k, c.k.nc
    kcT, vcT, ebig, ostage = c.big[0], c.big[1], c.big[2], c.big[3]
    kcTb, vcTb, ebigb, ostageb = c.bigb
    ksT = k.sb("ksT", [64, S], BF16)
    kwT = k.sb("kwT", [64, S], BF16)
    ksTb, kwTb = k.buf("ksT"), k.buf("kwT")
    qti = [k.sb(f"qti{i}", [64, 4, 128], BF16) for i in range(2)]
    qtib = [k.buf("qti") for _ in range(2)]
    vsa = k.sb("vsa", [128, 64, 65], BF16)
    vwa = k.sb("vwa", [128, 64, 65], BF16)
    vsab, vwab = k.buf("vsa"), k.buf("vwa")
    w1 = [c.scr[0:64, i * 4096:(i + 1) * 4096].rearrange("p (l m) -> p l m", m=128) for i in range(2)]
    w1b = [k.buf("cw1") for _ in range(2)]
    w2 = [k.sb(f"cw2_{i}", [128, 64], BF16) for i in range(2)]
    w2b = [k.buf("cw2") for _ in range(2)]
    pef = k.sb("pef", [64, 2, 32], F32)
    peb = k.sb("peb", [64, 2, 32], BF16)
    pefb, pebb = k.buf("pef"), k.buf("peb")
    cbias = k.sb("cbias", [128, 2], F32)
    cbiasb = k.buf("cbias")
    hid = [k.sb(f"hid{i}", [128, 512], BF16) for i in range(2)]
    hidb = [k.buf("hid") for _ in range(2)]
    kcmpT = k.sb("kcmpT", [64, 512], BF16)
    kcmpTb = k.buf("kcmpT")
    cvo = k.sb("cvo", [128, 4, 193], BF16)
    cvob = k.buf("cvo")
    wm = k.sb("wm", [128, 6, 128], BF16)
    wmb = k.buf("wm")
    wm4 = k.sb("wm4", [128, 6, 4, 128], BF16)
    wm4b = k.buf("wm4")
    mci = [k.sb(f"mci{i}", [128, 4, 128], BF16) for i in range(2)]
    mcib = [k.buf("mci") for _ in range(2)]
    mc4 = [k.sb(f"mc4{i}", [128, 4, 4, 128], BF16) for i in range(2)]
    mc4b = [k.buf("mc4") for _ in range(2)]
    vmi = [k.sb(f"vmi{i}", [128, 128], F32) for i in range(2)]
    fbi = [k.sb(f"fbi{i}", [128, 128], F32) for i in range(2)]
    gti = [k.sb(f"gti{i}", [128, 12], F32) for i in range(2)]
    ldb = [k.buf("ld") for _ in range(2)]
    PTc = k.sb("PTc", [128, 4, 512], BF16)
    PTcb = [k.buf("PTc") for _ in range(4)]
    PT = [k.sb(f"PT{i}", [128, 512], BF16) for i in range(3)]
    PTb = [k.buf("PT") for _ in range(3)]
    sm = k.sb("sm", [128, 64], F32)
    smb = k.buf("sm")
    imp = k.sb("imp", [128, 128], F32)
    sc = k.sb("sc", [128, 128], F32)
    sc2 = k.sb("sc2", [128, 128], F32)
    impb, scb, sc2b = k.buf("imp"), k.buf("sc"), k.buf("sc2")
    m8 = k.sb("m8", [128, 16], F32)
    m8b = k.buf("m8")
    nm = k.sb("nm", [128, 128], BF16)
    nmb = k.buf("nm")
    nm4 = k.sb("nm4", [128, 4, 128], BF16)
    nm4b = k.buf("nm4")
    ocs = k.sb("ocs", [128, 4, 64], F32)
    ocsb = k.buf("ocs")
    of = k.sb("of", [128, 4, 64], F32)
    ofb = k.buf("of")
    ob = k.sb("ob", [128, 256], BF16)
    obb = k.buf("ob")

    for (dst, dstb, name) in ((ksT, ksTb, "ksT"), (kwT, kwTb, "kwT"), (kcT, kcTb, "kcT"), (vcT, vcTb, "vcT")):
        for h2 in range(2):
            k.dma(k.sp, lambda: nc.sync.dma_start(out=dst[0:64, h2 * 4096:(h2 + 1) * 4096],
                                                  in_=d[name][:, h2 * 4096:(h2 + 1) * 4096]), writes=[dstb])
    for h2 in range(2):
        k.dma(k.sp, lambda: nc.sync.dma_start(out=ebig[:, h2 * 4096:(h2 + 1) * 4096], in_=d["ebig"][:, h2 * 4096:(h2 + 1) * 4096]),
              writes=[ebigb])
    vv = d["vsw"].rearrange("(kt p) c -> p kt c", p=128)
    k.op(k.pool, lambda: nc.gpsimd.memset(vsa[:, :, 64:65], 1.0), writes=[vsab])
    k.op(k.pool, lambda: nc.gpsimd.memset(vwa[:, :, 64:65], 1.0), writes=[vwab])
    for q4 in range(4):
        k.dma(k.pool, lambda: nc.gpsimd.dma_start(out=vsa[:, q4 * 16:(q4 + 1) * 16, 0:64], in_=vv[:, q4 * 16:(q4 + 1) * 16, 0:64]),
              writes=[vsab])
        k.dma(k.pool, lambda: nc.gpsimd.dma_start(out=vwa[:, q4 * 16:(q4 + 1) * 16, 0:64], in_=vv[:, q4 * 16:(q4 + 1) * 16, 64:128]),
              writes=[vwab])
    for kv in range(2):
        k.dma(k.pool, lambda: nc.gpsimd.dma_start(out=w1[kv], in_=d["cw1"][kv].rearrange("(l dd) m -> dd l m", dd=64)),
              writes=[w1b[kv]] + c.scr_bufs)
        k.dma(k.pool, lambda: nc.gpsimd.dma_start(out=w2[kv][:], in_=d["cw2"][kv]), writes=[w2b[kv]])
        k.dma(k.sp, lambda: nc.sync.dma_start(out=pef[:, kv, :], in_=d["peT"][kv]), writes=[pefb])
    k.op(k.dve, lambda: nc.vector.tensor_copy(out=peb[:], in_=pef[:]), reads=[pefb], writes=[pebb])
    k.dma(k.sp, lambda: nc.sync.dma_start(out=wm[:], in_=d["wm"][:, :, :]), writes=[wmb])
    k.op(k.dve, lambda: nc.vector.tensor_copy(out=wm4[:], in_=wm[:].unsqueeze(2).broadcast_to([128, 6, 4, 128])),
         reads=[wmb], writes=[wm4b])
    k.op(k.pool, lambda: nc.gpsimd.memset(cvo[:], 0.0), writes=[cvob])
    k.op(k.pool, lambda: nc.gpsimd.memset(cvo[:, :, 64:65], 1.0), writes=[cvob])
    k.dma(k.sp, lambda: nc.sync.dma_start(out=cvo[:, :, 65:193], in_=d["ov"][:, :, :]), writes=[cvob])
    k.op(k.pool, lambda: nc.gpsimd.memset(kcmpT[:], 0.0), writes=[kcmpTb])

    import os
    stage = int(os.environ.get("NSA_STAGE", "99"))
    nqb_run = int(os.environ.get("NSA_NQB", str(NQB)))
    if stage < 1:
        return
    for kv in range(2):
        src = (kcT, vcT)[kv]
        srcb = (kcTb, vcTb)[kv]
        sv = src[0:64, :].rearrange("dd (n s) -> dd n s", s=16)
        for l in range(32):
            rhs = sv[:, 0:511, l] if l < 16 else sv[:, 1:512, l - 16]
            k.op(k.pe, lambda: nc.tensor.matmul(c.bank(7, 511), lhsT=w1[kv][:, l, :], rhs=rhs, start=(l == 0), stop=(l == 31)),
                 reads=[w1b[kv], srcb], writes=[c.psb[7]])
        for l in range(32):
            k.op(k.pe, lambda: nc.tensor.matmul(c.bank(6, 1), lhsT=w1[kv][:, l, :], rhs=peb[:, kv, l:l + 1],
                                                start=(l == 0), stop=(l == 31)),
                 reads=[w1b[kv], pebb], writes=[c.psb[6]])
        k.op(k.act, lambda: nc.scalar.copy(out=cbias[:, kv:kv + 1], in_=c.bank(6, 1)), reads=[c.psb[6]], writes=[cbiasb])
        k.op(k.act, lambda: nc.scalar.activation(out=hid[kv][:, 0:511], in_=c.bank(7, 511), func=AF.Silu,
                                                 bias=cbias[:, kv:kv + 1]),
             reads=[c.psb[7], cbiasb], writes=[hidb[kv]])
    k.op(k.pe, lambda: nc.tensor.matmul(c.pst[0:64, 7 * 512:7 * 512 + 511], lhsT=w2[0][:], rhs=hid[0][:, 0:511], start=True, stop=True),
         reads=[w2b[0], hidb[0]], writes=[c.psb[7]])
    k.op(k.act, lambda: nc.scalar.copy(out=kcmpT[:, 0:511], in_=c.pst[0:64, 7 * 512:7 * 512 + 511]),
         reads=[c.psb[7]], writes=[kcmpTb])
    for nt in range(4):
        nr = 128 if nt < 3 else 127
        k.op(k.pe, lambda: nc.tensor.matmul(c.pst[0:nr, 6 * 512 + nt * 64:6 * 512 + (nt + 1) * 64], lhsT=hid[1][:, nt * 128:nt * 128 + nr],
                                            rhs=w2[1][:], start=True, stop=True),
             reads=[w2b[1], hidb[1]], writes=[c.psb[6]])
        k.op(k.act, lambda: nc.scalar.copy(out=cvo[0:nr, nt, 0:64], in_=c.pst[0:nr, 6 * 512 + nt * 64:6 * 512 + (nt + 1) * 64]),
             reads=[c.psb[6]], writes=[cvob])

    if stage < 2:
        return
    S0, C0, OS, OW, TB = 0, 2, 4, 5, 6
    nsc = 0
    npt = 0
    Cv = c.pst[:, C0 * 512:C0 * 512 + 1024].rearrange("p (g x) -> p g x", x=256)
    Osv = c.bank(OS).rearrange("p (g x) -> p g x", x=128)
    Owv = c.bank(OW).rearrange("p (g x) -> p g x", x=128)
    for i in range(nqb_run):
        li = i % 2
        k.dma(k.sp, lambda: nc.sync.dma_start(out=mci[li][:], in_=d["maskc"][i]), writes=[mcib[li]])
        k.dma(k.sp, lambda: nc.sync.dma_start(out=vmi[li][:], in_=d["vm"][i]), writes=[ldb[li]])
        k.dma(k.sp, lambda: nc.sync.dma_start(out=fbi[li][:], in_=d["fb"][i]), writes=[ldb[li]])
        k.dma(k.sp, lambda: nc.sync.dma_start(out=gti[li][:], in_=d["gts"][i * 128:(i + 1) * 128, :]), writes=[ldb[li]])
        k.op(k.pool, lambda: nc.gpsimd.tensor_copy(out=mc4[li][:], in_=mci[li][:].unsqueeze(2).broadcast_to([128, 4, 4, 128])),
             reads=[mcib[li]], writes=[mc4b[li]])
        k.dma(k.sp, lambda: nc.sync.dma_start(out=qti[li][:], in_=d["qT"][:, :, i * 128:(i + 1) * 128]), writes=[qtib[li]])
        QTi = qti[li][:].rearrange("p g q -> p (g q)")
        QTb = qtib[li]
        for nt in range(4):
            sb_ = S0 + (nsc % 2)
            nsc += 1
            k.op(k.pe, lambda: nc.tensor.matmul(c.bank(sb_), lhsT=c.ident[:], rhs=mc4[li][:, nt].rearrange("p g q -> p (g q)"),
                                                start=True, stop=False),
                 reads=[c.identb, mc4b[li]], writes=[c.psb[sb_]])
            k.op(k.pe, lambda: nc.tensor.matmul(c.bank(sb_), lhsT=kcmpT[:, nt * 128:(nt + 1) * 128], rhs=QTi, start=False, stop=True),
                 reads=[kcmpTb, QTb], writes=[c.psb[sb_]])
            k.op(k.act, lambda: nc.scalar.activation(out=PTc[:, nt, :], in_=c.bank(sb_), func=AF.Exp, scale=0.125),
                 reads=[c.psb[sb_]], writes=[PTcb[nt]])
        for g in range(4):
            bnk = C0 + g // 2
            for nt in range(4):
                k.op(k.pe, lambda: nc.tensor.matmul(Cv[:, g, 0:193], lhsT=PTc[:, nt, g * 128:(g + 1) * 128], rhs=cvo[:, nt, :],
                                                    start=(nt == 0 and g % 2 == 0), stop=(nt == 3), skip_group_check=True),
                     reads=[PTcb[nt], cvob], writes=[c.psb[bnk]])
        if stage < 3:
            continue
        zc, rzc = sm[:, 0:4], sm[:, 4:8]
        k.op(k.dve, lambda: nc.vector.tensor_scalar(out=zc, in0=Cv[:, :, 64], scalar1=c.cst[:, 2:3], scalar2=None, op0=ALU.add),
             reads=[c.psb[C0], c.psb[C0 + 1]], writes=[smb])
        k.op(k.dve, lambda: nc.vector.reciprocal(out=rzc, in_=zc), reads=[smb], writes=[smb])
        k.op(k.dve, lambda: nc.vector.tensor_scalar(out=imp[:], in0=Cv[:, 0, 65:193], scalar1=sm[:, 4:5], scalar2=None, op0=ALU.mult),
             reads=[c.psb[C0], smb], writes=[impb])
        for g in range(1, 4):
            k.op(k.dve, lambda: nc.vector.scalar_tensor_tensor(out=imp[:], in0=Cv[:, g, 65:193], scalar=sm[:, 4 + g:5 + g], in1=imp[:],
                                                               op0=ALU.mult, op1=ALU.add),
                 reads=[c.psb[C0 + g // 2], smb, impb], writes=[impb])
        k.op(k.act, lambda: nc.scalar.copy(out=ocs[:], in_=Cv[:, :, 0:64]), reads=[c.psb[C0], c.psb[C0 + 1]], writes=[ocsb])
        sub = int(os.environ.get("NSA_SUB", "99"))
        if sub < 1:
            continue
        k.op(k.dve, lambda: nc.vector.tensor_tensor(out=sc[:], in0=imp[:], in1=vmi[li][:], op=ALU.mult),
             reads=[impb, ldb[li]], writes=[scb])
        k.op(k.dve, lambda: nc.vector.tensor_tensor(out=sc[:], in0=sc[:], in1=fbi[li][:], op=ALU.add),
             reads=[scb, ldb[li]], writes=[scb])
        if sub < 2:
            continue
        k.op(k.dve, lambda: nc.vector.max(out=m8[:, 0:8], in_=sc[:]), reads=[scb], writes=[m8b])
        k.op(k.dve, lambda: nc.vector.match_replace(out=sc2[:], in_to_replace=m8[:, 0:8], in_values=sc[:], imm_value=-1e30),
             reads=[scb, m8b], writes=[sc2b])
        k.op(k.dve, lambda: nc.vector.max(out=m8[:, 8:16], in_=sc2[:]), reads=[sc2b], writes=[m8b])
        if sub < 3:
            continue
        k.op(k.dve, lambda: nc.vector.tensor_scalar(out=nm[:], in0=sc[:], scalar1=m8[:, 15:16], scalar2=NEG, op0=ALU.is_lt, op1=ALU.mult),
             reads=[scb, m8b], writes=[nmb])
        tbf = c.bank(TB).bitcast(BF16)
        k.op(k.pe, lambda: nc.tensor.transpose(out=tbf[:, 0:128], in_=nm[:], identity=c.ident[:]),
             reads=[nmb, c.identb], writes=[c.psb[TB]])
        if sub < 4:
            continue
        k.op(k.act, lambda: nc.scalar.copy(out=nm4[:], in_=tbf[:, 0:128].unsqueeze(1).broadcast_to([128, 4, 128])),
             reads=[c.psb[TB]], writes=[nm4b])
        if stage < 4:
            continue
        nkt = 2 * i + 2
        for kt in range(nkt):
            sb_ = S0 + (nsc % 2)
            nsc += 1
            k.op(k.pe, lambda: nc.tensor.matmul(c.bank(sb_), lhsT=ebig[:, kt * 128:(kt + 1) * 128], rhs=nm4[:].rearrange("p g q -> p (g q)"),
                                                start=True, stop=False),
                 reads=[ebigb, nm4b], writes=[c.psb[sb_]])
            if kt >= 2 * i:
                j = 4 + kt - 2 * i
                k.op(k.pe, lambda: nc.tensor.matmul(c.bank(sb_), lhsT=c.ident[:], rhs=wm4[:, j].rearrange("p g q -> p (g q)"),
                                                    start=False, stop=False),
                     reads=[c.identb, wm4b], writes=[c.psb[sb_]])
            k.op(k.pe, lambda: nc.tensor.matmul(c.bank(sb_), lhsT=ksT[:, kt * 128:(kt + 1) * 128], rhs=QTi, start=False, stop=True),
                 reads=[ksTb, QTb], writes=[c.psb[sb_]])
            pi = npt % 3
            npt += 1
            k.op(k.act, lambda: nc.scalar.activation(out=PT[pi][:], in_=c.bank(sb_), func=AF.Exp, scale=0.125),
                 reads=[c.psb[sb_]], writes=[PTb[pi]])
            for g in range(4):
                k.op(k.pe, lambda: nc.tensor.matmul(Osv[:, g, 0:65], lhsT=PT[pi][:, g * 128:(g + 1) * 128], rhs=vsa[:, kt, :],
                                                    start=(kt == 0 and g == 0), stop=(kt == nkt - 1), skip_group_check=True),
                     reads=[PTb[pi], vsab], writes=[c.psb[OS]])
        if stage < 5:
            continue
        kts = [(j, 2 * i - 4 + j) for j in range(6) if 2 * i - 4 + j >= 0]
        for idx, (j, kt) in enumerate(kts):
            sb_ = S0 + (nsc % 2)
            nsc += 1
            k.op(k.pe, lambda: nc.tensor.matmul(c.bank(sb_), lhsT=c.ident[:], rhs=wm4[:, j].rearrange("p g q -> p (g q)"),
                                                start=True, stop=False),
                 reads=[c.identb, wm4b], writes=[c.psb[sb_]])
            k.op(k.pe, lambda: nc.tensor.matmul(c.bank(sb_), lhsT=kwT[:, kt * 128:(kt + 1) * 128], rhs=QTi, start=False, stop=True),
                 reads=[kwTb, QTb], writes=[c.psb[sb_]])
            pi = npt % 3
            npt += 1
            k.op(k.act, lambda: nc.scalar.activation(out=PT[pi][:], in_=c.bank(sb_), func=AF.Exp, scale=0.125),
                 reads=[c.psb[sb_]], writes=[PTb[pi]])
            for g in range(4):
                k.op(k.pe, lambda: nc.tensor.matmul(Owv[:, g, 0:65], lhsT=PT[pi][:, g * 128:(g + 1) * 128], rhs=vwa[:, kt, :],
                                                    start=(idx == 0 and g == 0), stop=(idx == len(kts) - 1), skip_group_check=True),
                     reads=[PTb[pi], vwab], writes=[c.psb[OW]])
        if stage < 6:
            continue
        gs = sm[:, 8:20]
        gs3 = gs.rearrange("p (g j) -> p g j", j=3)
        zs, rzs, zw, rzw = sm[:, 20:24], sm[:, 24:28], sm[:, 28:32], sm[:, 32:36]
        cc, cs, cw = sm[:, 36:40], sm[:, 40:44], sm[:, 44:48]
        k.op(k.act, lambda: nc.scalar.activation(out=gs, in_=gti[li][:], func=AF.Sigmoid), reads=[ldb[li]], writes=[smb])
        k.op(k.dve, lambda: nc.vector.tensor_scalar(out=zs, in0=Osv[:, :, 64], scalar1=c.cst[:, 2:3], scalar2=None, op0=ALU.add),
             reads=[c.psb[OS]], writes=[smb])
        k.op(k.dve, lambda: nc.vector.reciprocal(out=rzs, in_=zs), reads=[smb], writes=[smb])
        k.op(k.dve, lambda: nc.vector.tensor_scalar(out=zw, in0=Owv[:, :, 64], scalar1=c.cst[:, 2:3], scalar2=None, op0=ALU.add),
             reads=[c.psb[OW]], writes=[smb])
        k.op(k.dve, lambda: nc.vector.reciprocal(out=rzw, in_=zw), reads=[smb], writes=[smb])
        k.op(k.dve, lambda: nc.vector.tensor_tensor(out=cc, in0=gs3[:, :, 0], in1=rzc, op=ALU.mult), reads=[smb], writes=[smb])
        k.op(k.dve, lambda: nc.vector.tensor_tensor(out=cs, in0=gs3[:, :, 1], in1=rzs, op=ALU.mult), reads=[smb], writes=[smb])
        k.op(k.dve, lambda: nc.vector.tensor_tensor(out=cw, in0=gs3[:, :, 2], in1=rzw, op=ALU.mult), reads=[smb], writes=[smb])
        for g in range(4):
            k.op(k.dve, lambda: nc.vector.tensor_scalar(out=of[:, g, :], in0=ocs[:, g, :], scalar1=sm[:, 36 + g:37 + g], scalar2=None,
                                                        op0=ALU.mult),
                 reads=[ocsb, smb], writes=[ofb])
            k.op(k.dve, lambda: nc.vector.scalar_tensor_tensor(out=of[:, g, :], in0=Osv[:, g, 0:64], scalar=sm[:, 40 + g:41 + g],
                                                               in1=of[:, g, :], op0=ALU.mult, op1=ALU.add),
                 reads=[c.psb[OS], smb, ofb], writes=[ofb])
            k.op(k.dve, lambda: nc.vector.scalar_tensor_tensor(out=ob[:, g * 64:(g + 1) * 64], in0=Owv[:, g, 0:64],
                                                               scalar=sm[:, 44 + g:45 + g], in1=of[:, g, :], op0=ALU.mult, op1=ALU.add),
                 reads=[c.psb[OW], smb, ofb], writes=[obb])
        for h2 in range(2):
            k.op(k.pe, lambda: nc.tensor.transpose(out=tbf[:, 128 + h2 * 128:256 + h2 * 128], in_=ob[:, h2 * 128:(h2 + 1) * 128],
                                                   identity=c.ident[:]),
                 reads=[obb, c.identb], writes=[c.psb[TB]])
        k.op(k.act, lambda: nc.scalar.copy(out=ostage[:, 0:2 * NQ].rearrange("p (h q) -> p h q", h=2)[:, :, i * 128:(i + 1) * 128],
                                           in_=tbf[:, 128:384].rearrange("p (h q) -> p h q", h=2)),
             reads=[c.psb[TB]], writes=[ostageb])
    k.dma(k.sp, lambda: nc.sync.dma_start(out=d["onT"].rearrange("(h p) q -> p h q", p=128),
                                          in_=ostage[:, 0:2 * NQ].rearrange("p (h q) -> p h q", h=2)), reads=[ostageb])


def m_consts(par):
    bf = ml_dtypes.bfloat16
    q = np.arange(128)
    maskc = np.zeros((NQB, 128, 4, 128), np.float32)
    vm = np.zeros((NQB, 128, 128), np.float32)
    fb = np.zeros((NQB, 128, 128), np.float32)
    blk = np.arange(128)
    for i in range(NQB):
        qb = 2 * i + par
        t = qb * 128 + q
        for nt in range(4):
            n = nt * 128 + np.arange(128)
            valid = (n[:, None] <= 510) & (16 * n[:, None] + 31 <= t[None, :])
            maskc[i, :, nt, :] = np.where(valid, 0.0, NEG)
        cur = t // 64
        forced = (blk[None, :] == 0) | (blk[None, :] == cur[:, None]) | (blk[None, :] == cur[:, None] - 1)
        valid = blk[None, :] * 64 <= t[:, None]
        fb[i] = np.where(forced, 1e6, np.where(valid, 0.0, -1.0))
        vm[i] = np.where(valid & ~forced, 1.0, 0.0)
    wm = np.zeros((128, 6, 128), np.float32)
    kl = np.arange(128)
    for j in range(6):
        dist = (4 - j + par) * 128 + q[None, :] - kl[:, None]
        wm[:, j, :] = np.where((dist >= 0) & (dist < 512), 0.0, NEG)
    jc = np.arange(512)[:, None]
    js = np.arange(128)[None, :]
    ov = ((jc * 16 < (js + 1) * 64) & (jc * 16 + 32 > js * 64) & (jc <= 510)).astype(np.float32)
    ov = ov.reshape(4, 128, 128).transpose(1, 0, 2)
    ebig = (np.arange(S)[None, :] // 64 == np.arange(128)[:, None]).astype(np.float32)
    return {"maskc": maskc.astype(bf), "vm": vm, "fb": fb, "wm": wm.astype(bf), "ov": np.ascontiguousarray(ov).astype(bf),
            "ebig": ebig.astype(bf), "ident": ident_bf16()}


def build_M(do_nsa=True, do_gdn=True):
    nc = bass.Bass("TRN2", target_bir_lowering=False)
    d = {}

    def inp(name, shape, dt):
        d[name] = nc.dram_tensor(name, shape, dt, kind="ExternalInput").ap()

    inp("ident", [128, 128], BF16)
    if do_nsa:
        inp("qT", [64, 4, NQ], BF16)
        for n in ("ksT", "kwT", "kcT", "vcT"):
            inp(n, [64, S], BF16)
        inp("vsw", [S, 128], F32)
        inp("gts", [NQ, 12], F32)
        inp("maskc", [NQB, 128, 4, 128], BF16)
        inp("wm", [128, 6, 128], BF16)
        inp("vm", [NQB, 128, 128], F32)
        inp("fb", [NQB, 128, 128], F32)
        inp("ov", [128, 4, 128], BF16)
        inp("ebig", [128, S], BF16)
        inp("peT", [2, 64, 32], F32)
        inp("cw1", [2, 2048, 128], F32)
        inp("cw2", [2, 128, 64], F32)
        d["onT"] = nc.dram_tensor("onT", [256, NQ], BF16, kind="ExternalOutput").ap()
    if do_gdn:
        inp("gqkvT", [3, 128, S], BF16)
        inp("gzg", [S, 128], F32)
        inp("gab", [128, 2, 64], F32)
        inp("convw", [128, 12], F32)
        inp("gsc", [128, 2], F32)
        inp("gnw", [128, 128], F32)
        inp("gcst", [4, 128, 128], F32)
        inp("identf", [128, 128], F32)
        inp("gmask", [8, 128, 128], F32)
        d["ogT"] = nc.dram_tensor("ogT", [128, S], BF16, kind="ExternalOutput").ap()
        import os
        if os.environ.get("GDN_DBG"):
            d["dbg"] = nc.dram_tensor("dbg", [128, 16, 64], F32, kind="ExternalOutput").ap()
    k = K(nc)
    c = MCtx(k)
    emit_mconsts(c, d["ident"])
    if do_gdn:
        emit_gdn(c, d)
    if do_nsa:
        emit_nsa(c, d)
    k.finish([])
    return nc


def emit_gdn(c, d):
    k, nc = c.k, c.k.nc
    qT, kT, vT, ostage = c.big
    qTb, kTb, vTb, ostageb = c.bigb
    NCH = S // 128
    tok = k.sb("gtok", [128, 16, 64], F32)
    tokb = k.buf("gtok")
    (AB_A, AB_B, SP, G, BETA, NBETA, GC, NGC, EGC, GL, EGL, EKD, BEGC, DKG) = range(14)
    gsc = k.sb("gsc", [128, 4], F32)
    gscb = k.buf("gsc")
    convw = k.sb("convw", [128, 12], F32)
    convwb = k.buf("convw")
    gnw = k.sb("gnw", [128, 128], F32)
    gnwb = k.buf("gnw")
    gcst = k.sb("gcst", [128, 4, 128], F32)
    gcstb = k.buf("gcst")
    identf = k.sb("identf", [128, 128], F32)
    identfb = k.buf("identf")
    onesb = k.sb("onesb", [128, 128], BF16)
    onesbb = k.buf("onesb")
    raw = [c.scr[:, 0:1027], c.scr[:, 1032:2059]]
    rawb = [k.buf("graw") for _ in range(2)]
    acc = [c.scr[:, 2064:4112].bitcast(F32), c.scr[:, 4112:6160].bitcast(F32)]
    accb = [k.buf("gacc") for _ in range(2)]
    sq = c.scr[:, 6160:7184]
    sqb = k.buf("gsq")
    c.scr_bufs = rawb + accb + [sqb]
    sd = k.sb("gsd", [128, 512], F32)
    sdb = k.buf("gsd")
    rn = k.sb("grn", [128, 512], F32)
    rnb = k.buf("grn")
    solf = k.sb("solf", [128, 256], F32)
    solb = k.sb("solb", [128, 256], BF16)
    solfb, solbb = k.buf("solf"), k.buf("solb")
    dg = k.sb("dg", [128, 128], F32)
    dgb = k.buf("dg")
    Em = k.sb("Em", [128, 128], F32)
    Emb = k.buf("Em")
    Dm = k.sb("Dm", [128, 128], F32)
    Dmb = k.buf("Dm")
    t1 = k.sb("gt1", [128, 128], F32)
    t1b = k.buf("gt1")
    Nn = [k.sb(f"Nn{i}", [128, 128], BF16) for i in range(2)]
    NT = [k.sb(f"NT{i}", [128, 128], BF16) for i in range(2)]
    Nnb = [k.buf("Nn") for _ in range(2)]
    NTb = [k.buf("NT") for _ in range(2)]
    attn = k.sb("attn", [128, 128], BF16)
    attnb = k.buf("attn")
    Af = k.sb("Af", [128, 128], F32)
    ATf = k.sb("ATf", [128, 128], F32)
    Afb, ATfb = k.buf("Af"), k.buf("ATf")
    Pm = k.sb("Pm", [128, 128], F32)
    Pm2 = k.sb("Pm2", [128, 128], F32)
    Pmb, Pm2b = k.buf("Pm"), k.buf("Pm2")
    Tm = [k.sb(f"Tm{i}", [128, 128], F32) for i in range(2)]
    TTm = [k.sb(f"TTm{i}", [128, 128], F32) for i in range(2)]
    Tmb = [k.buf("Tm") for _ in range(2)]
    TTmb = [k.buf("TTm") for _ in range(2)]
    gmask = k.sb("gmask", [128, 8, 128], F32)
    gmaskb = k.buf("gmask")
    kdec = [k.sb(f"kdec{i}", [128, 128], BF16) for i in range(2)]
    attnT = [k.sb(f"attnT{i}", [128, 128], BF16) for i in range(2)]
    wT = [k.sb(f"wT{i}", [128, 128], BF16) for i in range(2)]
    uf = [k.sb(f"uf{i}", [128, 128], F32) for i in range(2)]
    kdecb = [k.buf("kdec") for _ in range(2)]
    attnTb = [k.buf("attnT") for _ in range(2)]
    wTb = [k.buf("wT") for _ in range(2)]
    ufb = [k.buf("uf") for _ in range(2)]
    Sf = k.sb("Sf", [128, 128], F32)
    Sb = k.sb("Sb", [128, 128], BF16)
    Sfb, Sbb = k.buf("Sf"), k.buf("Sb")
    vnew = k.sb("vnew", [128, 128], BF16)
    vnewb = k.buf("vnew")
    qs = k.sb("qs", [128, 128], F32)
    qsb = k.buf("qs")
    o_f = k.sb("o_f", [128, 128], F32)
    o_fb = k.buf("o_f")
    gjunk = k.sb("gjunk", [128, 128], BF16)
    gjunkb = k.buf("gjunk")
    gst = k.sb("gst", [128, 8], F32)
    gstb = k.buf("gst")
    zg = [k.sb(f"zg{i}", [128, 128], F32) for i in range(2)]
    zgb = [k.buf("zg") for _ in range(2)]
    sz = k.sb("sz", [128, 128], F32)
    szb = k.buf("sz")
    on = k.sb("on", [128, 128], F32)
    onb = k.buf("on")
    obf = k.sb("obf", [128, 128], BF16)
    obfb = k.buf("obf")

    k.dma(k.sp, lambda: nc.sync.dma_start(out=tok[:, 0:2, :], in_=d["gab"][:, :, :]), writes=[tokb])
    k.dma(k.sp, lambda: nc.sync.dma_start(out=gsc[:, 0:2], in_=d["gsc"][:, :]), writes=[gscb])
    k.dma(k.sp, lambda: nc.sync.dma_start(out=convw[:], in_=d["convw"][:, :]), writes=[convwb])
    k.dma(k.sp, lambda: nc.sync.dma_start(out=gnw[:], in_=d["gnw"][:, :]), writes=[gnwb])
    for i4 in range(4):
        k.dma(k.sp, lambda: nc.sync.dma_start(out=gcst[:, i4, :], in_=d["gcst"][i4]), writes=[gcstb])
    k.dma(k.sp, lambda: nc.sync.dma_start(out=identf[:], in_=d["identf"][:, :]), writes=[identfb])
    for i8 in range(8):
        k.dma(k.sp, lambda: nc.sync.dma_start(out=gmask[:, i8, :], in_=d["gmask"][i8]), writes=[gmaskb])
    k.op(k.pool, lambda: nc.gpsimd.memset(onesb[:], 1.0), writes=[onesbb])
    k.op(k.pool, lambda: nc.gpsimd.memset(Sf[:], 0.0), writes=[Sfb])
    k.op(k.pool, lambda: nc.gpsimd.memset(Sb[:], 0.0), writes=[Sbb])
    mtri, ones32, maskneg, strict01 = gcst[:, 0, :], gcst[:, 1, :], gcst[:, 2, :], gcst[:, 3, :]

    T = lambda j: tok[:, j, :]
    A = k.act
    k.op(A, lambda: nc.scalar.activation(out=T(SP), in_=T(AB_A), func=AF.Exp, bias=gsc[:, 1:2]), reads=[tokb, gscb], writes=[tokb])
    k.op(A, lambda: nc.scalar.activation(out=T(SP), in_=T(SP), func=AF.Ln, bias=c.cst[:, 1:2]), reads=[tokb], writes=[tokb])
    k.op(A, lambda: nc.scalar.activation(out=gsc[:, 2:3], in_=gsc[:, 0:1], func=AF.Exp), reads=[gscb], writes=[gscb])
    k.op(k.dve, lambda: nc.vector.tensor_scalar(out=T(G), in0=T(SP), scalar1=gsc[:, 2:3], scalar2=-1.0, op0=ALU.mult, op1=ALU.mult),
         reads=[tokb, gscb], writes=[tokb])
    k.op(A, lambda: nc.scalar.activation(out=T(BETA), in_=T(AB_B), func=AF.Sigmoid), reads=[tokb], writes=[tokb])
    k.op(k.dve, lambda: nc.vector.tensor_scalar(out=T(NBETA), in0=T(BETA), scalar1=-1.0, scalar2=None, op0=ALU.mult),
         reads=[tokb], writes=[tokb])
    k.op(k.pe, lambda: nc.tensor.matmul(c.bank(0, 64), lhsT=mtri, rhs=T(G), start=True, stop=True), reads=[gcstb, tokb], writes=[c.psb[0]])
    k.op(k.pe, lambda: nc.tensor.matmul(c.bank(0, 64, 64), lhsT=ones32, rhs=T(G), start=True, stop=True), reads=[gcstb, tokb],
         writes=[c.psb[0]])
    k.op(A, lambda: nc.scalar.copy(out=T(GC), in_=c.bank(0, 64)), reads=[c.psb[0]], writes=[tokb])
    k.op(A, lambda: nc.scalar.copy(out=T(GL), in_=c.bank(0, 64, 64)), reads=[c.psb[0]], writes=[tokb])
    k.op(k.dve, lambda: nc.vector.tensor_scalar(out=T(NGC), in0=T(GC), scalar1=-1.0, scalar2=None, op0=ALU.mult), reads=[tokb], writes=[tokb])
    k.op(k.dve, lambda: nc.vector.tensor_tensor(out=T(DKG), in0=T(GL), in1=T(GC), op=ALU.subtract), reads=[tokb], writes=[tokb])
    k.op(A, lambda: nc.scalar.activation(out=T(EGC), in_=T(GC), func=AF.Exp), reads=[tokb], writes=[tokb])
    k.op(A, lambda: nc.scalar.activation(out=T(EGL), in_=T(GL), func=AF.Exp), reads=[tokb], writes=[tokb])
    k.op(A, lambda: nc.scalar.activation(out=T(EKD), in_=T(DKG), func=AF.Exp), reads=[tokb], writes=[tokb])
    k.op(k.dve, lambda: nc.vector.tensor_tensor(out=T(BEGC), in0=T(BETA), in1=T(EGC), op=ALU.mult), reads=[tokb], writes=[tokb])
    col = lambda j, n: tok[:, j, n:n + 1]
    if "dbg" in d:
        k.dma(k.sp, lambda: nc.sync.dma_start(out=d["dbg"][:, :, :], in_=tok[:]), reads=[tokb])

    ri = 0
    for pc in range(8):
        p0 = pc * 1024
        for which in range(3):
            r_, rb_ = raw[ri % 2], rawb[ri % 2]
            a_, ab_ = acc[ri % 2], accb[ri % 2]
            ri += 1
            if pc == 0:
                k.op(k.pool, lambda: nc.gpsimd.memset(r_[:, 0:3], 0.0), writes=[rb_])
                k.dma(k.sp, lambda: nc.sync.dma_start(out=r_[:, 3:1027], in_=d["gqkvT"][which][:, 0:1024]), writes=[rb_])
            else:
                k.dma(k.sp, lambda: nc.sync.dma_start(out=r_[:, 0:1027], in_=d["gqkvT"][which][:, p0 - 3:p0 + 1024]), writes=[rb_])
            k.op(k.dve, lambda: nc.vector.tensor_scalar(out=a_[:], in0=r_[:, 3:1027], scalar1=convw[:, which * 4 + 3:which * 4 + 4],
                                                        scalar2=None, op0=ALU.mult),
                 reads=[rb_, convwb], writes=[ab_])
            for j in range(3):
                eng = k.dve
                ee = nc.vector
                k.op(eng, lambda: ee.scalar_tensor_tensor(out=a_[:], in0=r_[:, j:j + 1024], scalar=convw[:, which * 4 + j:which * 4 + j + 1],
                                                          in1=a_[:], op0=ALU.mult, op1=ALU.add),
                     reads=[rb_, convwb, ab_], writes=[ab_])
            if which == 2:
                k.op(A, lambda: nc.scalar.activation(out=vT[:, p0:p0 + 1024], in_=a_[:], func=AF.Silu), reads=[ab_], writes=[vTb])
                continue
            dst, dstb = (qT, qTb) if which == 0 else (kT, kTb)
            scl = 128.0 ** -0.5 if which == 0 else 1.0
            k.op(A, lambda: nc.scalar.activation(out=a_[:], in_=a_[:], func=AF.Silu), reads=[ab_], writes=[ab_])
            k.op(k.pool, lambda: nc.gpsimd.tensor_tensor(out=sq[:], in0=a_[:], in1=a_[:], op=ALU.mult), reads=[ab_], writes=[sqb])
            for hh in range(2):
                k.op(k.pe, lambda: nc.tensor.matmul(c.bank(1), lhsT=onesb[:], rhs=sq[:, hh * 512:(hh + 1) * 512], start=True, stop=True),
                     reads=[onesbb, sqb], writes=[c.psb[1]])
                k.op(A, lambda: nc.scalar.activation(out=sd[:], in_=c.bank(1), func=AF.Sqrt, bias=c.cst[:, 0:1]),
                     reads=[c.psb[1]], writes=[sdb])
                k.op(k.dve, lambda: nc.vector.reciprocal(out=rn[:], in_=sd[:]), reads=[sdb], writes=[rnb])
                k.op(k.dve, lambda: nc.vector.scalar_tensor_tensor(out=dst[:, p0 + hh * 512:p0 + (hh + 1) * 512],
                                                                   in0=a_[:, hh * 512:(hh + 1) * 512], scalar=float(scl), in1=rn[:],
                                                                   op0=ALU.mult, op1=ALU.mult),
                     reads=[ab_, rnb], writes=[dstb])

    bA, bB, bC, bD, bE, bF, bG, bH = range(8)
    tA = c.bank(bA).bitcast(BF16)
    tD = c.bank(bD).bitcast(BF16)

    def pre(n):
        sl = n % 2
        cs = slice(n * 128, (n + 1) * 128)
        k.op(k.pe, lambda: nc.tensor.transpose(out=tA[:, 0:128], in_=kT[:, cs], identity=c.ident[:]), reads=[kTb, c.identb],
             writes=[c.psb[bA]])
        k.op(k.pe, lambda: nc.tensor.transpose(out=tA[:, 128:256], in_=vT[:, cs], identity=c.ident[:]), reads=[vTb, c.identb],
             writes=[c.psb[bA]])
        k.op(k.dve, lambda: nc.vector.tensor_scalar(out=solf[:, 0:128], in0=tA[:, 128:256], scalar1=col(BETA, n), scalar2=None, op0=ALU.mult),
             reads=[c.psb[bA], tokb], writes=[solfb])
        k.op(k.dve, lambda: nc.vector.tensor_scalar(out=solf[:, 128:256], in0=tA[:, 0:128], scalar1=col(BEGC, n), scalar2=None, op0=ALU.mult),
             reads=[c.psb[bA], tokb], writes=[solfb])
        k.op(A, lambda: nc.scalar.activation(out=kdec[sl][:], in_=tA[:, 0:128], func=AF.Copy, scale=col(EKD, n)),
             reads=[c.psb[bA], tokb], writes=[kdecb[sl]])
        k.op(k.pe, lambda: nc.tensor.matmul(c.bank(bB, 128), lhsT=kT[:, cs], rhs=kT[:, cs], start=True, stop=True), reads=[kTb],
             writes=[c.psb[bB]])
        k.op(k.pe, lambda: nc.tensor.matmul(c.bank(bB, 128, 128), lhsT=qT[:, cs], rhs=kT[:, cs], start=True, stop=True), reads=[kTb, qTb],
             writes=[c.psb[bB]])
        k.op(k.dve, lambda: nc.vector.tensor_scalar(out=dg[:], in0=identf[:], scalar1=col(NGC, n), scalar2=None, op0=ALU.mult),
             reads=[identfb, tokb], writes=[dgb])
        k.op(k.pe, lambda: nc.tensor.matmul(c.bank(bC, 128), lhsT=ones32, rhs=dg[:], start=True, stop=True), reads=[gcstb, dgb],
             writes=[c.psb[bC]])
        k.op(k.dve, lambda: nc.vector.scalar_tensor_tensor(out=Em[:], in0=c.bank(bC, 128), scalar=col(GC, n), in1=maskneg,
                                                           op0=ALU.add, op1=ALU.add),
             reads=[c.psb[bC], tokb, gcstb], writes=[Emb])
        k.op(A, lambda: nc.scalar.activation(out=Dm[:], in_=Em[:], func=AF.Exp), reads=[Emb], writes=[Dmb])
        k.op(k.dve, lambda: nc.vector.tensor_tensor(out=t1[:], in0=c.bank(bB, 128), in1=Dm[:], op=ALU.mult), reads=[c.psb[bB], Dmb],
             writes=[t1b])
        k.op(k.dve, lambda: nc.vector.scalar_tensor_tensor(out=Af[:], in0=t1[:], scalar=col(BETA, n), in1=strict01,
                                                           op0=ALU.mult, op1=ALU.mult),
             reads=[t1b, tokb, gcstb], writes=[Afb])
        k.op(k.dve, lambda: nc.vector.tensor_tensor(out=attn[:], in0=c.bank(bB, 128, 128), in1=Dm[:], op=ALU.mult),
             reads=[c.psb[bB], Dmb], writes=[attnb])
        k.op(k.pe, lambda: nc.tensor.transpose(out=c.bank(bC, 128), in_=Af[:], identity=identf[:]), reads=[Afb, identfb],
             writes=[c.psb[bC]])
        k.op(A, lambda: nc.scalar.copy(out=ATf[:], in_=c.bank(bC, 128)), reads=[c.psb[bC]], writes=[ATfb])
        k.op(k.pe, lambda: nc.tensor.transpose(out=tD[:, 128:256], in_=attn[:], identity=c.ident[:]), reads=[attnb, c.identb],
             writes=[c.psb[bD]])
        k.op(A, lambda: nc.scalar.copy(out=attnT[sl][:], in_=tD[:, 128:256]), reads=[c.psb[bD]], writes=[attnTb[sl]])
        k.op(k.dve, lambda: nc.vector.tensor_tensor(out=Pm[:], in0=Af[:], in1=gmask[:, 0, :], op=ALU.mult), reads=[Afb, gmaskb], writes=[Pmb])
        k.op(k.dve, lambda: nc.vector.tensor_tensor(out=Tm[0][:], in0=identf[:], in1=Pm[:], op=ALU.subtract), reads=[identfb, Pmb],
             writes=[Tmb[0]])
        k.op(k.pool, lambda: nc.gpsimd.tensor_tensor(out=Pm2[:], in0=ATf[:], in1=gmask[:, 7, :], op=ALU.mult), reads=[ATfb, gmaskb],
             writes=[Pm2b])
        k.op(k.pool, lambda: nc.gpsimd.tensor_tensor(out=TTm[0][:], in0=identf[:], in1=Pm2[:], op=ALU.subtract), reads=[identfb, Pm2b],
             writes=[TTmb[0]])
        cur = 0
        for lev in range(1, 7):
            nx = 1 - cur
            k.op(k.pe, lambda: nc.tensor.matmul(c.bank(bE, 128), lhsT=ATf[:], rhs=Tm[cur][:], start=True, stop=True),
                 reads=[ATfb, Tmb[cur]], writes=[c.psb[bE]])
            k.op(k.dve, lambda: nc.vector.tensor_tensor(out=Pm[:], in0=c.bank(bE, 128), in1=gmask[:, lev, :], op=ALU.mult),
                 reads=[c.psb[bE], gmaskb], writes=[Pmb])
            k.op(k.pe, lambda: nc.tensor.matmul(c.bank(bF, 128), lhsT=TTm[cur][:], rhs=Pm[:], start=True, stop=True),
                 reads=[TTmb[cur], Pmb], writes=[c.psb[bF]])
            k.op(k.pe, lambda: nc.tensor.matmul(c.bank(bF, 128, 128), lhsT=Pm[:], rhs=TTm[cur][:], start=True, stop=True),
                 reads=[TTmb[cur], Pmb], writes=[c.psb[bF]])
            k.op(k.dve, lambda: nc.vector.tensor_tensor(out=Tm[nx][:], in0=Tm[cur][:], in1=c.bank(bF, 128), op=ALU.subtract),
                 reads=[Tmb[cur], c.psb[bF]], writes=[Tmb[nx]])
            k.op(k.dve, lambda: nc.vector.tensor_tensor(out=TTm[nx][:], in0=TTm[cur][:], in1=c.bank(bF, 128, 128), op=ALU.subtract),
                 reads=[TTmb[cur], c.psb[bF]], writes=[TTmb[nx]])
            cur = nx
        k.op(k.pe, lambda: nc.tensor.matmul(c.bank(bE, 256), lhsT=TTm[cur][:], rhs=solf[:], start=True, stop=True),
             reads=[TTmb[cur], solfb], writes=[c.psb[bE]])
        k.op(k.dve, lambda: nc.vector.tensor_copy(out=uf[sl][:], in_=c.bank(bE, 128)), reads=[c.psb[bE]], writes=[ufb[sl]])
        k.op(A, lambda: nc.scalar.copy(out=solb[:, 128:256], in_=c.bank(bE, 128, 128)), reads=[c.psb[bE]], writes=[solbb])
        k.op(k.pe, lambda: nc.tensor.transpose(out=tD[:, 256:384], in_=solb[:, 128:256], identity=c.ident[:]), reads=[solbb, c.identb],
             writes=[c.psb[bD]])
        k.op(A, lambda: nc.scalar.copy(out=wT[sl][:], in_=tD[:, 256:384]), reads=[c.psb[bD]], writes=[wTb[sl]])

    def scan(n):
        sl = n % 2
        cs = slice(n * 128, (n + 1) * 128)
        k.dma(k.sp, lambda: nc.sync.dma_start(out=zg[sl][:], in_=d["gzg"][n * 128:(n + 1) * 128, :]), writes=[zgb[sl]])
        k.op(k.pe, lambda: nc.tensor.matmul(c.bank(bG, 128), lhsT=wT[sl][:], rhs=Sb[:], start=True, stop=True), reads=[wTb[sl], Sbb],
             writes=[c.psb[bG]])
        k.op(k.pe, lambda: nc.tensor.matmul(c.bank(bG, 128, 128), lhsT=qT[:, cs], rhs=Sb[:], start=True, stop=True), reads=[qTb, Sbb],
             writes=[c.psb[bG]])
        k.op(k.dve, lambda: nc.vector.tensor_tensor(out=vnew[:], in0=uf[sl][:], in1=c.bank(bG, 128), op=ALU.subtract),
             reads=[ufb[sl], c.psb[bG]], writes=[vnewb])
        k.op(A, lambda: nc.scalar.activation(out=qs[:], in_=c.bank(bG, 128, 128), func=AF.Copy, scale=col(EGC, n)),
             reads=[c.psb[bG], tokb], writes=[qsb])
        k.op(k.pe, lambda: nc.tensor.matmul(c.bank(bH, 128), lhsT=attnT[sl][:], rhs=vnew[:], start=True, stop=True),
             reads=[attnTb[sl], vnewb], writes=[c.psb[bH]])
        k.op(k.pe, lambda: nc.tensor.matmul(c.bank(bH, 128, 128), lhsT=kdec[sl][:], rhs=vnew[:], start=True, stop=True),
             reads=[kdecb[sl], vnewb], writes=[c.psb[bH]])
        k.op(k.dve, lambda: nc.vector.scalar_tensor_tensor(out=Sf[:], in0=Sf[:], scalar=col(EGL, n), in1=c.bank(bH, 128, 128),
                                                           op0=ALU.mult, op1=ALU.add),
             reads=[Sfb, tokb, c.psb[bH]], writes=[Sfb])
        k.op(A, lambda: nc.scalar.copy(out=Sb[:], in_=Sf[:]), reads=[Sfb], writes=[Sbb])
        k.op(k.dve, lambda: nc.vector.tensor_tensor(out=o_f[:], in0=qs[:], in1=c.bank(bH, 128), op=ALU.add), reads=[qsb, c.psb[bH]],
             writes=[o_fb])
        k.op(A, lambda: nc.scalar.activation(out=gjunk[:], in_=o_f[:], func=AF.Square, accum_out=gst[:, 0:1]), reads=[o_fb],
             writes=[gjunkb, gstb])
        k.op(A, lambda: nc.scalar.activation(out=gst[:, 1:2], in_=gst[:, 0:1], func=AF.Sqrt, scale=1.0 / 128.0, bias=c.cst[:, 0:1]),
             reads=[gstb], writes=[gstb])
        k.op(k.dve, lambda: nc.vector.reciprocal(out=gst[:, 2:3], in_=gst[:, 1:2]), reads=[gstb], writes=[gstb])
        k.op(A, lambda: nc.scalar.activation(out=sz[:], in_=zg[sl][:], func=AF.Silu), reads=[zgb[sl]], writes=[szb])
        k.op(k.dve, lambda: nc.vector.scalar_tensor_tensor(out=on[:], in0=o_f[:], scalar=gst[:, 2:3], in1=gnw[:], op0=ALU.mult, op1=ALU.mult),
             reads=[o_fb, gstb, gnwb], writes=[onb])
        k.op(k.pool, lambda: nc.gpsimd.tensor_tensor(out=obf[:], in0=on[:], in1=sz[:], op=ALU.mult), reads=[onb, szb], writes=[obfb])
        k.op(k.pe, lambda: nc.tensor.transpose(out=tD[:, 384:512], in_=obf[:], identity=c.ident[:]), reads=[obfb, c.identb],
             writes=[c.psb[bD]])
        k.op(A, lambda: nc.scalar.copy(out=ostage[:, cs], in_=tD[:, 384:512]), reads=[c.psb[bD]], writes=[ostageb])

    import os
    nch_run = int(os.environ.get("GDN_NCH", str(NCH)))
    pre(0)
    for n in range(nch_run):
        if n + 1 < nch_run:
            pre(n + 1)
        scan(n)
    for h2 in range(2):
        k.dma(k.sp, lambda: nc.sync.dma_start(out=d["ogT"][:, h2 * 4096:(h2 + 1) * 4096], in_=ostage[:, h2 * 4096:(h2 + 1) * 4096]),
              reads=[ostageb])


def gdn_consts():
    i = np.arange(128)
    mtri = (i[:, None] <= i[None, :]).astype(np.float32)
    ones = np.ones((128, 128), np.float32)
    maskneg = np.where(i[:, None] >= i[None, :], 0.0, NEG).astype(np.float32)
    strict = (i[:, None] > i[None, :]).astype(np.float32)
    masks = []
    for l in range(7):
        sblk = 1 << l
        m = ((i[:, None] // (2 * sblk) == i[None, :] // (2 * sblk)) & ((i[:, None] % (2 * sblk)) >= sblk)
             & ((i[None, :] % (2 * sblk)) < sblk)).astype(np.float32)
        masks.append(m)
    masks.append(np.ascontiguousarray(masks[0].T))
    return {"gcst": np.stack([mtri, ones, maskneg, strict]), "identf": np.eye(128, dtype=np.float32),
            "gmask": np.stack(masks)}


def get_prog(name):
    if name not in _PROGS:
        _PROGS[name] = {"TA": build_TA, "TB": build_TB, "M": build_M}[name]()
    return _PROGS[name]


def _tokidx(par):
    return np.concatenate([np.arange((2 * i + par) * 128, (2 * i + par + 1) * 128) for i in range(NQB)])


def _run(name, in_maps):
    res = run_bass_kernel_spmd(get_prog(name), in_maps, core_ids=list(range(NCORES)))
    return res.results


def kernel(x, p, ffn1_norm, ffn1_w1, ffn1_w3, ffn1_w2, mix_norm, w_in, cmp_pe_k, cmp_pe_v,
           cmp_k_w1, cmp_k_w2, cmp_v_w1, cmp_v_w2, gdn_conv, gdn_a_log, gdn_dt_bias, gdn_norm,
           w_out, ffn2_norm, ffn2_w1, ffn2_w3, ffn2_w2, ple_norm, ple_gate, ple_proj, final_norm):
    f32 = np.float32
    A = lambda a: np.ascontiguousarray(np.asarray(a, dtype=f32))
    x = A(x).reshape(B * S, D)
    p = A(p).reshape(DEPTH, B * S, 256)
    cosT, sinT = rope_tables_fm()
    ident = ident_bf16()
    mc = [m_consts(0), m_consts(1)]
    gc = gdn_consts()
    tix = [_tokidx(0), _tokidx(1)]
    xcur = [np.ascontiguousarray(x[c * TOK:(c + 1) * TOK]) for c in range(NCORES)]
    cos_c = [np.ascontiguousarray(cosT[:, (c % 4) * TOK:(c % 4 + 1) * TOK]) for c in range(NCORES)]
    sin_c = [np.ascontiguousarray(sinT[:, (c % 4) * TOK:(c % 4 + 1) * TOK]) for c in range(NCORES)]
    out = None
    for l in range(DEPTH):
        wA = {"ffn_norm": A(ffn1_norm[l]), "w1": A(ffn1_w1[l]), "w3": A(ffn1_w3[l]), "w2": A(ffn1_w2[l]),
              "mix_norm": A(mix_norm[l]), "w_in": A(w_in[l]), "ident": ident}
        rA = _run("TA", [dict(wA, x=xcur[c], cosT=cos_c[c], sinT=sin_c[c]) for c in range(NCORES)])
        xcur = [rA[c]["x_out"] for c in range(NCORES)]
        zfm_b = [np.concatenate([rA[b * 4 + j]["zfm"] for j in range(4)], axis=1) for b in range(B)]
        ztm_b = [np.concatenate([rA[b * 4 + j]["ztm"] for j in range(4)], axis=0) for b in range(B)]
        conv_l = A(gdn_conv[l])
        wM = {"peT": np.ascontiguousarray(np.stack([A(cmp_pe_k[l]).T, A(cmp_pe_v[l]).T])),
              "cw1": np.ascontiguousarray(np.stack([A(cmp_k_w1[l]), A(cmp_v_w1[l])])),
              "cw2": np.ascontiguousarray(np.stack([A(cmp_k_w2[l]), A(cmp_v_w2[l])])),
              "gnw": np.ascontiguousarray(np.tile(A(gdn_norm[l])[None, :], (128, 1)))}
        wM.update(gc)
        in_maps = []
        for c in range(NCORES):
            b, hkv, par, gh = c // 4, (c % 4) // 2, c % 2, c % 4
            zf, zt = zfm_b[b], ztm_b[b]
            m = dict(wM)
            m.update(mc[par])
            q = zf[hkv * 256:(hkv + 1) * 256][:, tix[par]]
            m["qT"] = np.ascontiguousarray(q.reshape(4, 64, NQ).transpose(1, 0, 2))
            m["kcT"] = np.ascontiguousarray(zf[512 + hkv * 64:512 + (hkv + 1) * 64])
            m["vcT"] = np.ascontiguousarray(zf[640 + hkv * 64:640 + (hkv + 1) * 64])
            m["ksT"] = np.ascontiguousarray(zf[768 + hkv * 64:768 + (hkv + 1) * 64])
            m["kwT"] = np.ascontiguousarray(zf[896 + hkv * 64:896 + (hkv + 1) * 64])
            m["vsw"] = np.ascontiguousarray(np.concatenate([zt[:, hkv * 64:(hkv + 1) * 64],
                                                            zt[:, 128 + hkv * 64:128 + (hkv + 1) * 64]], axis=1))
            m["gts"] = np.ascontiguousarray(zt[tix[par]][:, 256 + hkv * 12:256 + (hkv + 1) * 12])
            m["gqkvT"] = np.ascontiguousarray(np.stack([zf[1024 + w * 512 + gh * 128:1024 + w * 512 + (gh + 1) * 128]
                                                        for w in range(3)]))
            m["gzg"] = np.ascontiguousarray(zt[:, 280 + gh * 128:280 + (gh + 1) * 128])
            m["gab"] = np.ascontiguousarray(np.stack([zt[:, 792 + gh].reshape(64, 128).T,
                                                      zt[:, 796 + gh].reshape(64, 128).T], axis=1))
            m["convw"] = np.ascontiguousarray(np.concatenate(
                [conv_l[:, w * 512 + gh * 128:w * 512 + (gh + 1) * 128].T for w in range(3)], axis=1))
            m["gsc"] = np.ascontiguousarray(np.tile(np.stack([A(gdn_a_log[l])[gh], A(gdn_dt_bias[l])[gh]])[None, :], (128, 1)))
            in_maps.append(m)
        rM = _run("M", in_maps)
        om_b = []
        for b in range(B):
            om = np.zeros((D, S), dtype=ml_dtypes.bfloat16)
            for j in range(4):
                c = b * 4 + j
                hkv, par, gh = j // 2, j % 2, j
                om[hkv * 256:(hkv + 1) * 256][:, tix[par]] = rM[c]["onT"]
                om[512 + gh * 128:512 + (gh + 1) * 128] = rM[c]["ogT"]
            om_b.append(om)
        wB = {"w_out": A(w_out[l]), "ffn_norm": A(ffn2_norm[l]), "w1": A(ffn2_w1[l]), "w3": A(ffn2_w3[l]),
              "w2": A(ffn2_w2[l]), "ple_norm": A(ple_norm[l]), "ple_gate": A(ple_gate[l]), "ple_proj": A(ple_proj[l]),
              "final_norm": A(final_norm), "ident": ident}
        rB = _run("TB", [dict(wB, x=xcur[c], omixT=np.ascontiguousarray(om_b[c // 4][:, (c % 4) * TOK:(c % 4 + 1) * TOK]),
                              p=np.ascontiguousarray(p[l][c * TOK:(c + 1) * TOK])) for c in range(NCORES)])
        xcur = [rB[c]["x_out"] for c in range(NCORES)]
        if l == DEPTH - 1:
            out = np.concatenate([rB[c]["xn_out"] for c in range(NCORES)], axis=0)
    return np.ascontiguousarray(out.reshape(B, S, D).astype(np.float32))
```
